# Optimizing a Trainium2 kernel written in Bass

```python
import math
import jax, jax.numpy as jnp
from jax import lax
import numpy as np

D_MODEL = 2048
BATCH = 4
SEQ = 4096
DEPTH = 2

MIX_WIDTH = D_MODEL
HY_WIDTH = MIX_WIDTH // 2
HY_ORDER = 2
HY_DIRS = 2
HY_EMB = 33
HY_BANDS = (HY_EMB - 1) // 2
HY_FFN = 64
HY_RATE_MIN = 3.07
HY_RATE_MAX = 15.35
SHORT_CONV = 3
GLA_WIDTH = MIX_WIDTH - HY_WIDTH
GLA_HEADS = 4
GLA_DK = GLA_WIDTH // (2 * GLA_HEADS)
GLA_DV = GLA_WIDTH // GLA_HEADS
GLA_RANK = 16
GLA_TAU = 16.0
GLA_CHUNK = 64
RW_HEAD = 64
RW_HEADS = D_MODEL // RW_HEAD
RW_DECAY_LORA = 96
RW_ICLR_LORA = 96
RW_GATE_LORA = 256
RW_LOG_DECAY_MAX = 0.606531
RW_GN_EPS = 64e-5
D_FF = 4 * D_MODEL
NORM_EPS = 1e-6
N_EVEN = (DEPTH + 1) // 2
N_ODD = DEPTH // 2
PROJ_SPLITS = (3 * HY_WIDTH, GLA_HEADS * GLA_DK, GLA_HEADS * GLA_DK, GLA_WIDTH, GLA_WIDTH, GLA_RANK, GLA_RANK)
IN_WIDTH = sum(PROJ_SPLITS)

kernel_name = 'hybrid_hyena_gla_rwkv7_encoder'


def rmsnorm(x, g, eps=NORM_EPS):
    xf = x.astype(jnp.float32)
    y = xf * lax.rsqrt(jnp.mean(xf * xf, axis=-1, keepdims=True) + eps)
    return (y * g.astype(jnp.float32)).astype(x.dtype)


def adaln(x, c, norm_g, ada_w, ada_b):
    m = jax.nn.silu(c) @ ada_w + ada_b
    shift, scale, gate = jnp.split(m[:, None, :], 3, axis=-1)
    return rmsnorm(x, norm_g) * (1.0 + scale) + shift, gate


def short_conv_centred(u, w, b):
    pad = SHORT_CONV // 2
    L = u.shape[1]
    up = jnp.pad(u, ((0, 0), (pad, pad), (0, 0)))
    return sum(up[:, j:j + L] * w[j] for j in range(SHORT_CONV)) + b


def hyena_filter_spectra(L, w1, b1, freq1, w2, b2, freq2, w3, decay):
    t = jnp.linspace(0.0, 1.0, L, dtype=jnp.float32)[:, None]
    ang = (2.0 * math.pi / L) * jnp.arange(L, dtype=jnp.float32)[:, None]
    bands = jnp.linspace(1e-4, HY_BANDS - 1, HY_BANDS, dtype=jnp.float32)[None, :]
    z = jnp.concatenate([t, jnp.cos(bands * ang), -jnp.sin(bands * ang)], axis=-1)
    f = jnp.sin(freq1 * (z @ w1 + b1))
    f = jnp.sin(freq2 * (f @ w2 + b2))
    f = (f @ w3).astype(jnp.float32) * jnp.exp(-t * jnp.abs(decay.astype(jnp.float32)))
    f = f.reshape(L, HY_ORDER, HY_DIRS, HY_WIDTH)
    fwd, bwd = f[:, :, 0], f[:, :, 1]
    k_circ = jnp.concatenate([fwd, jnp.zeros_like(fwd[:1]), bwd[:0:-1]], axis=0)
    return jnp.fft.rfft(k_circ, axis=0)


def fft_long_conv(u, k_f, skip):
    L = u.shape[1]
    uf = u.astype(jnp.float32)
    y = jnp.fft.irfft(jnp.fft.rfft(uf, n=2 * L, axis=1) * k_f, n=2 * L, axis=1)[:, :L]
    return (y + uf * skip.astype(jnp.float32)).astype(u.dtype)


def hyena_mixer(u_proj, conv_w, conv_b, f_w1, f_b1, f_freq1, f_w2, f_b2, f_freq2, f_w3, f_decay, skip):
    u = short_conv_centred(u_proj, conv_w, conv_b)
    v, x1, x2 = jnp.split(u, 3, axis=-1)
    k_f = hyena_filter_spectra(u.shape[1], f_w1, f_b1, f_freq1, f_w2, f_b2, f_freq2, f_w3, f_decay)
    z = x1 * fft_long_conv(v, k_f[:, 0], skip[0])
    return x2 * fft_long_conv(z, k_f[:, 1], skip[1])


def gla_chunked_scan(q, k, v, g):
    B, L, H, dk = q.shape
    dv = v.shape[-1]
    n_chunks = L // GLA_CHUNK

    def to_chunks(a):
        return a.reshape(B, n_chunks, GLA_CHUNK, H, a.shape[-1]).transpose(1, 0, 3, 2, 4)

    qc, kc, vc, gc = to_chunks(q), to_chunks(k), to_chunks(v), to_chunks(g)
    vc = vc.astype(jnp.float32)
    b = jnp.cumsum(gc.astype(jnp.float32), axis=3)
    b_last = b[:, :, :, -1:]
    q_in = qc * jnp.exp(b)
    k_in = kc * jnp.exp(-b)
    k_end = kc * jnp.exp(b_last - b)
    mask = jnp.tril(jnp.ones((GLA_CHUNK, GLA_CHUNK), dtype=bool))
    att = jnp.where(mask, jnp.einsum('nbhid,nbhjd->nbhij', q_in, k_in), 0.0)
    o_intra = jnp.einsum('nbhij,nbhjv->nbhiv', att, vc)

    def step(state, inp):
        q_n, k_n, v_n, dec_n = inp
        o_n = jnp.einsum('bhid,bhdv->bhiv', q_n, state)
        state = state * dec_n[..., None] + jnp.einsum('bhjd,bhjv->bhdv', k_n, v_n)
        return state, o_n

    s0 = jnp.zeros((B, H, dk, dv), jnp.float32)
    _, o_inter = lax.scan(step, s0, (q_in, k_end, vc, jnp.exp(b_last[:, :, :, 0])))
    o = o_intra + o_inter
    return o.transpose(1, 0, 3, 2, 4).reshape(B, L, H, dv).astype(v.dtype)


def gla_mixer(q, k, v, r, lr_f, lr_b, up, up_b, norm_g):
    B, L, _ = q.shape
    q = q.reshape(B, L, GLA_HEADS, GLA_DK) * GLA_DK ** -0.5
    k = k.reshape(B, L, GLA_HEADS, GLA_DK)
    v = v.reshape(B, L, GLA_HEADS, GLA_DV)

    def log_gate(lr, d):
        a = jax.nn.log_sigmoid((lr @ up[d] + up_b[d]).astype(jnp.float32)) / GLA_TAU
        return a.reshape(B, L, GLA_HEADS, GLA_DK)

    flip = lambda a: jnp.flip(a, axis=1)
    o_f = gla_chunked_scan(q, k, v, log_gate(lr_f, 0))
    o_b = flip(gla_chunked_scan(flip(q), flip(k), flip(v), flip(log_gate(lr_b, 1))))
    o = rmsnorm(o_f + o_b, norm_g.reshape(GLA_HEADS, GLA_DV))
    return o.reshape(B, L, GLA_WIDTH) * jax.nn.silu(r)


def hyena_gla_block(h, w_in, w_out, conv_w, conv_b, f_w1, f_b1, f_freq1, f_w2, f_b2, f_freq2, f_w3,
                    f_decay, hy_skip, gla_up, gla_up_b, gla_norm_g):
    p = h @ w_in
    cuts = [int(s) for s in np.cumsum(PROJ_SPLITS)[:-1]]
    hy_in, q, k, v, r, lr_f, lr_b = jnp.split(p, cuts, axis=-1)
    y_hy = hyena_mixer(hy_in, conv_w, conv_b, f_w1, f_b1, f_freq1, f_w2, f_b2, f_freq2, f_w3, f_decay, hy_skip)
    y_gla = gla_mixer(q, k, v, r, lr_f, lr_b, gla_up, gla_up_b, gla_norm_g)
    return jnp.concatenate([y_hy, y_gla.astype(y_hy.dtype)], axis=-1) @ w_out


def rwkv7_scan(r, logw, k, v, kk, a):
    B, L, H, N = r.shape
    tm = lambda a: jnp.moveaxis(a.astype(jnp.float32), 1, 0)

    def step(state, inp):
        r_t, w_t, k_t, v_t, kk_t, a_t = inp
        sa = jnp.einsum('bhvk,bhk->bhv', state, -kk_t)
        state = (state * w_t[:, :, None, :] + sa[..., None] * (kk_t * a_t)[:, :, None, :]
                 + v_t[..., None] * k_t[:, :, None, :])
        return state, jnp.einsum('bhvk,bhk->bhv', state, r_t)

    s0 = jnp.zeros((B, H, N, N), jnp.float32)
    _, o = lax.scan(step, s0, (tm(r), tm(jnp.exp(logw)), tm(k), tm(v), tm(kk), tm(a)))
    return jnp.moveaxis(o, 0, 1)


def rwkv7_block(h, mu, w_r, w_k, w_v, w0, w1, w2, a0, a1, a2, g1, g2, k_k, k_a, r_k, gn_g, gn_b, w_o):
    B, L, D = h.shape
    heads = lambda a: a.reshape(a.shape[:-1] + (RW_HEADS, RW_HEAD))
    hp = jnp.pad(h, ((0, 0), (1, 1), (0, 0)))
    xx = 0.5 * (hp[:, :-2] + hp[:, 2:]) - h
    xr, xw, xk, xv, xa, xg = [h + xx * mu[j] for j in range(6)]
    r = heads(xr @ w_r).astype(jnp.float32)
    k = heads(xk @ w_k).astype(jnp.float32)
    v = heads(xv @ w_v).astype(jnp.float32)
    g = jax.nn.sigmoid(xg @ g1) @ g2
    kk = k * heads(k_k).astype(jnp.float32)
    kk = kk * lax.rsqrt(jnp.maximum(jnp.sum(kk * kk, axis=-1, keepdims=True), 1e-24))
    flip = lambda a: jnp.flip(a, axis=1)
    state_out = jnp.zeros((B, L, RW_HEADS, RW_HEAD), jnp.float32)
    bonus = jnp.zeros((B, L, RW_HEADS, RW_HEAD), jnp.float32)
    for d in range(2):
        logw = heads(-RW_LOG_DECAY_MAX * jax.nn.sigmoid((w0[d] + jnp.tanh(xw @ w1[d]) @ w2[d]).astype(jnp.float32)))
        a = heads(jax.nn.sigmoid((a0[d] + (xa @ a1[d]) @ a2[d]).astype(jnp.float32)))
        k_d = k * (1.0 + (a - 1.0) * heads(k_a).astype(jnp.float32))
        seqs = (r, logw, k_d, v, kk, a)
        if d == 0:
            o_d = rwkv7_scan(*seqs)
        else:
            o_d = flip(rwkv7_scan(*[flip(s) for s in seqs]))
        state_out = state_out + o_d
        bonus = bonus + jnp.sum(r * k_d * r_k.astype(jnp.float32), axis=-1, keepdims=True) * v
    mean = jnp.mean(state_out, axis=-1, keepdims=True)
    var = jnp.mean(jnp.square(state_out - mean), axis=-1, keepdims=True)
    o = ((state_out - mean) * lax.rsqrt(var + RW_GN_EPS) * heads(gn_g).astype(jnp.float32)
         + heads(gn_b).astype(jnp.float32))
    o = (o + bonus).reshape(B, L, D).astype(h.dtype) * g
    return o @ w_o


def sqrelu_mlp(h, w1, w2):
    return jnp.square(jax.nn.relu(h @ w1)) @ w2


def setup_inputs(seed: int = 0) -> dict:
    key = jax.random.key(seed)
    keys = iter(jax.random.split(key, 64))

    def nrm(shape, scale):
        return scale * jax.random.normal(next(keys), shape, jnp.float32)

    def uni(shape, lo, hi):
        return jax.random.uniform(next(keys), shape, jnp.float32, lo, hi)

    def gain(shape):
        return 1.0 + nrm(shape, 0.02)

    D, NE, NO = D_MODEL, N_EVEN, N_ODD
    HYC = HY_ORDER * HY_DIRS * HY_WIDTH
    return {
        'x': nrm((BATCH, SEQ, D), 1.0),
        'c': nrm((BATCH, D), 1.0),
        'mix_norm_g': gain((NE, D)),
        'mix_ada_w': nrm((NE, D, 3 * D), 0.5 * D ** -0.5),
        'mix_ada_b': nrm((NE, 3 * D), 0.02),
        'ab_w_in': nrm((NE, D, IN_WIDTH), D ** -0.5),
        'ab_w_out': nrm((NE, MIX_WIDTH, D), MIX_WIDTH ** -0.5),
        'hy_conv_w': nrm((NE, SHORT_CONV, 3 * HY_WIDTH), SHORT_CONV ** -0.5),
        'hy_conv_b': nrm((NE, 3 * HY_WIDTH), 0.02),
        'hy_ffn_w1': nrm((NE, HY_EMB, HY_FFN), HY_EMB ** -0.5),
        'hy_ffn_b1': nrm((NE, HY_FFN), 0.1),
        'hy_freq1': 1.0 + nrm((NE, HY_FFN), 0.1),
        'hy_ffn_w2': nrm((NE, HY_FFN, HY_FFN), HY_FFN ** -0.5),
        'hy_ffn_b2': nrm((NE, HY_FFN), 0.1),
        'hy_freq2': 1.0 + nrm((NE, HY_FFN), 0.1),
        'hy_ffn_w3': nrm((NE, HY_FFN, HYC), 0.05 * HY_FFN ** -0.5),
        'hy_decay': uni((NE, HYC), HY_RATE_MIN, HY_RATE_MAX),
        'hy_skip': nrm((NE, HY_ORDER, HY_WIDTH), 0.5),
        'gla_up': nrm((NE, 2, GLA_RANK, GLA_HEADS * GLA_DK), GLA_RANK ** -0.5),
        'gla_up_b': uni((NE, 2, GLA_HEADS * GLA_DK), 0.0, 3.0),
        'gla_norm_g': gain((NE, GLA_WIDTH)),
        'tm_norm_g': gain((NO, D)),
        'tm_ada_w': nrm((NO, D, 3 * D), 0.5 * D ** -0.5),
        'tm_ada_b': nrm((NO, 3 * D), 0.02),
        'rw_mu': uni((NO, 6, D), 0.0, 1.0),
        'rw_w_r': nrm((NO, D, D), D ** -0.5),
        'rw_w_k': nrm((NO, D, D), D ** -0.5),
        'rw_w_v': nrm((NO, D, D), D ** -0.5),
        'rw_w0': uni((NO, 2, D), -3.0, 1.0),
        'rw_w1': nrm((NO, 2, D, RW_DECAY_LORA), D ** -0.5),
        'rw_w2': nrm((NO, 2, RW_DECAY_LORA, D), RW_DECAY_LORA ** -0.5),
        'rw_a0': nrm((NO, 2, D), 0.5),
        'rw_a1': nrm((NO, 2, D, RW_ICLR_LORA), D ** -0.5),
        'rw_a2': nrm((NO, 2, RW_ICLR_LORA, D), RW_ICLR_LORA ** -0.5),
        'rw_g1': nrm((NO, D, RW_GATE_LORA), D ** -0.5),
        'rw_g2': nrm((NO, RW_GATE_LORA, D), RW_GATE_LORA ** -0.5),
        'rw_k_k': 0.85 + nrm((NO, D), 0.05),
        'rw_k_a': 1.0 + nrm((NO, D), 0.05),
        'rw_r_k': nrm((NO, RW_HEADS, RW_HEAD), 0.1),
        'rw_gn_g': gain((NO, D)),
        'rw_gn_b': nrm((NO, D), 0.02),
        'rw_w_o': nrm((NO, D, D), D ** -0.5),
        'ffn_norm_g': gain((DEPTH, D)),
        'ffn_ada_w': nrm((DEPTH, D, 3 * D), 0.5 * D ** -0.5),
        'ffn_ada_b': nrm((DEPTH, 3 * D), 0.02),
        'ffn_w1': nrm((DEPTH, D, D_FF), D ** -0.5),
        'ffn_w2': nrm((DEPTH, D_FF, D), D_FF ** -0.5),
        'final_norm_g': gain((D,)),
    }


def reference(x, c, mix_norm_g, mix_ada_w, mix_ada_b, ab_w_in, ab_w_out, hy_conv_w, hy_conv_b,
              hy_ffn_w1, hy_ffn_b1, hy_freq1, hy_ffn_w2, hy_ffn_b2, hy_freq2, hy_ffn_w3, hy_decay, hy_skip,
              gla_up, gla_up_b, gla_norm_g, tm_norm_g, tm_ada_w, tm_ada_b, rw_mu, rw_w_r, rw_w_k, rw_w_v,
              rw_w0, rw_w1, rw_w2, rw_a0, rw_a1, rw_a2, rw_g1, rw_g2, rw_k_k, rw_k_a, rw_r_k, rw_gn_g, rw_gn_b,
              rw_w_o, ffn_norm_g, ffn_ada_w, ffn_ada_b, ffn_w1, ffn_w2, final_norm_g):
    for layer in range(DEPTH):
        i = layer // 2
        if layer % 2 == 0:
            h, gate = adaln(x, c, mix_norm_g[i], mix_ada_w[i], mix_ada_b[i])
            y = hyena_gla_block(h, ab_w_in[i], ab_w_out[i], hy_conv_w[i], hy_conv_b[i],
                                hy_ffn_w1[i], hy_ffn_b1[i], hy_freq1[i], hy_ffn_w2[i], hy_ffn_b2[i],
                                hy_freq2[i], hy_ffn_w3[i], hy_decay[i], hy_skip[i],
                                gla_up[i], gla_up_b[i], gla_norm_g[i])
        else:
            h, gate = adaln(x, c, tm_norm_g[i], tm_ada_w[i], tm_ada_b[i])
            y = rwkv7_block(h, rw_mu[i], rw_w_r[i], rw_w_k[i], rw_w_v[i], rw_w0[i], rw_w1[i], rw_w2[i],
                            rw_a0[i], rw_a1[i], rw_a2[i], rw_g1[i], rw_g2[i], rw_k_k[i], rw_k_a[i],
                            rw_r_k[i], rw_gn_g[i], rw_gn_b[i], rw_w_o[i])
        x = x + gate * y
        h, gate = adaln(x, c, ffn_norm_g[layer], ffn_ada_w[layer], ffn_ada_b[layer])
        x = x + gate * sqrelu_mlp(h, ffn_w1[layer], ffn_w2[layer])
    return rmsnorm(x, final_norm_g)
```

```python
import numpy as np
from contextlib import ExitStack
import ml_dtypes
import concourse.bass as bass
import concourse.mybir as mybir
from concourse.bass_utils import run_bass_kernel_spmd

F32 = mybir.dt.float32
BF16 = mybir.dt.bfloat16
AF = mybir.ActivationFunctionType
ALU = mybir.AluOpType
NPBF = ml_dtypes.bfloat16

D = 2048
L = 4096
DFF = 8192
EPS = 1e-6


class Buf:
    __slots__ = ("t", "w", "r", "excl")

    def __init__(self, t, excl=False):
        self.t = t
        self.w = None
        self.r = {}
        self.excl = excl

    def __getitem__(self, idx):
        return self.t[idx]


class KB:
    EPOCH = 12000
    NDS = 12

    def __init__(self, nc, es):
        self.nc = nc
        self.es = es
        self.root_es = es
        self.eng = {"pe": nc.tensor, "act": nc.scalar, "dve": nc.vector, "pool": nc.gpsimd, "sp": nc.sync}
        self.cnt = {e: 0 for e in self.eng}
        self.sems = {e: [] for e in self.eng}
        self.seen = {e: {} for e in self.eng}
        self.dq = {}
        self.nd = {}
        self.nbuf = 0
        self.out_tokens = []

    def _sem(self, name):
        return self.root_es.enter_context(self.nc.semaphore(name))

    def push(self):
        if not hasattr(self, "_es_stack"):
            self._es_stack = []
        self._es_stack.append(self.es)
        self.es = ExitStack()

    def pop(self):
        self.barrier()
        self.es.close()
        self.es = self._es_stack.pop()

    def sb(self, shape, dtype, name=None):
        self.nbuf += 1
        return Buf(self.es.enter_context(self.nc.sbuf_tensor(name or f"sb{self.nbuf}", list(shape), dtype)))

    def ps(self, shape, dtype=F32, name=None):
        self.nbuf += 1
        return Buf(self.es.enter_context(self.nc.psum_tensor(name or f"ps{self.nbuf}", list(shape), dtype)), excl=True)

    def _wait(self, e, tok):
        if tok is None:
            return
        if tok[0] == "e":
            _, f, n = tok
            if f == e and e in ("pe", "sp"):
                return
            ep = (n - 1) // self.EPOCH
            val = n - ep * self.EPOCH
            key = ("e", f, ep)
            sem = self.sems[f][ep]
        else:
            _, q, slot, c = tok
            key = ("d", q, slot)
            val = 16 * c
            sem = self.dq[q][slot][0]
        if self.seen[e].get(key, 0) >= val:
            return
        self.eng[e].wait_ge(sem, val)
        self.seen[e][key] = val

    def _deps(self, e, reads, writes):
        for b in reads:
            self._wait(e, b.w)
            if b.excl:
                for f, tok in b.r.items():
                    if f != e:
                        self._wait(e, tok)
        for b in writes:
            self._wait(e, b.w)
            for f, tok in b.r.items():
                if f != e:
                    self._wait(e, tok)

    def _mark(self, tok, f, reads, writes):
        for b in writes:
            b.w = tok
            b.r = {}
        for b in reads:
            b.r[f] = tok

    def op(self, e, fn, reads=(), writes=()):
        self._deps(e, reads, writes)
        n = self.cnt[e] + 1
        ep = (n - 1) // self.EPOCH
        while len(self.sems[e]) <= ep:
            self.sems[e].append(self._sem(f"s_{e}_{len(self.sems[e])}"))
        ins = fn(self.eng[e])
        ins.then_inc(self.sems[e][ep], 1)
        self.cnt[e] = n
        tok = ("e", e, n)
        self._mark(tok, e, reads, writes)
        return tok

    def dma(self, q, out, in_, reads=(), writes=(), is_output=False, **kw):
        self._deps(q, reads, writes)
        pool = self.dq.setdefault(q, [])
        i = self.nd.get(q, 0)
        self.nd[q] = i + 1
        slot = i % self.NDS
        if len(pool) <= slot:
            pool.append([self._sem(f"d_{q}_{slot}"), 0])
        sem, c = pool[slot]
        if c > 0:
            self._wait(q, ("d", q, slot, c))
        ins = self.eng[q].dma_start(out=out, in_=in_, **kw)
        ins.then_inc(sem, 16)
        pool[slot][1] = c + 1
        tok = ("d", q, slot, c + 1)
        self._mark(tok, "dma_" + q + str(slot), reads, writes)
        if is_output:
            self.out_tokens.append(tok)
        return tok

    def finish(self):
        for tok in self.out_tokens:
            self._wait("sp", tok)

    def mm(self, ps, out, lhsT_b, lhsT, rhs_b, rhs, start, stop):
        rb = [lhsT_b] if lhsT_b is rhs_b else [lhsT_b, rhs_b]
        return self.op("pe", lambda en: en.matmul(out, lhsT=lhsT, rhs=rhs, start=start, stop=stop),
                       reads=rb, writes=[ps])

    def transpose(self, ps, out, in_b, in_, id_b, ident):
        return self.op("pe", lambda en: en.transpose(out, in_, ident), reads=[in_b, id_b], writes=[ps])

    def act(self, out_b, out, in_b, in_, func, bias=None, scale=None, extra_reads=(), e="act"):
        kw = {}
        if bias is not None:
            kw["bias"] = bias
        if scale is not None:
            kw["scale"] = scale
        rd = [in_b] + list(extra_reads)
        return self.op("act", lambda en: en.activation(out=out, in_=in_, func=func, **kw), reads=rd, writes=[out_b])

    def tt(self, e, out_b, out, a_b, a, b_b, b, op):
        rd = [a_b] if a_b is b_b else [a_b, b_b]
        return self.op(e, lambda en: en.tensor_tensor(out=out, in0=a, in1=b, op=op), reads=rd, writes=[out_b])

    def ts(self, e, out_b, out, a_b, a, s1, op0, s2=None, op1=None, extra_reads=()):
        kw = dict(out=out, in0=a, scalar1=s1, scalar2=s2, op0=op0)
        if op1 is not None:
            kw["op1"] = op1
        return self.op(e, lambda en: en.tensor_scalar(**kw), reads=[a_b] + list(extra_reads), writes=[out_b])

    def stt(self, out_b, out, a_b, a, scalar, b_b, b, op0, op1, extra_reads=()):
        rd = [a_b, b_b] + list(extra_reads)
        return self.op("dve", lambda en: en.scalar_tensor_tensor(out=out, in0=a, scalar=scalar, in1=b, op0=op0, op1=op1),
                       reads=rd, writes=[out_b])

    def copy(self, e, out_b, out, in_b, in_):
        if e == "act":
            return self.act(out_b, out, in_b, in_, AF.Copy)
        return self.op(e, lambda en: en.tensor_copy(out=out, in_=in_), reads=[in_b], writes=[out_b])

    def memset(self, e, out_b, out, val):
        return self.op(e, lambda en: en.memset(out, val), reads=[], writes=[out_b])


class Rot:
    def __init__(self, bufs):
        self.bufs = bufs
        self.i = 0

    def next(self):
        b = self.bufs[self.i % len(self.bufs)]
        self.i += 1
        return b


def new_nc():
    return bass.Bass("TRN2", target_bir_lowering=False)


MCOLS = 768


def build_mod():
    nc = new_nc()
    cT = nc.dram_tensor("cT", [D, 4], F32, kind="ExternalInput").ap()
    w = nc.dram_tensor("w", [4, D, MCOLS], F32, kind="ExternalInput").ap()
    bia = nc.dram_tensor("bia", [4, MCOLS], F32, kind="ExternalInput").ap()
    out = nc.dram_tensor("out", [4, MCOLS, 4], F32, kind="ExternalOutput").ap()
    with ExitStack() as es:
        k = KB(nc, es)
        ct = k.sb([128, 16, 4], F32)
        cs = k.sb([128, 16, 4], BF16)
        sg = k.sb([128, 16, 4], F32)
        bt = k.sb([128, 4, 6], F32)
        k.dma("sp", ct[:], cT.rearrange("(kc p) b -> p kc b", p=128), writes=[ct])
        with nc.allow_non_contiguous_dma(reason="tiny bias load"):
            k.dma("sp", bt[:], bia.rearrange("a (cc p) -> p a cc", p=128), writes=[bt])
        k.act(sg, sg[:], ct, ct[:], AF.Sigmoid)
        k.tt("dve", cs, cs[:], ct, ct[:], sg, sg[:], ALU.mult)
        wst = Rot([k.sb([128, 16, 128], F32) for _ in range(3)])
        wbf = Rot([k.sb([128, 16, 128], BF16) for _ in range(3)])
        pss = Rot([k.ps([128, 512]) for _ in range(2)])
        ot = Rot([k.sb([128, 4], F32) for _ in range(3)])
        i = 0
        for a in range(4):
            for cc in range(6):
                ws = wst.next()
                wb = wbf.next()
                k.dma("sp", ws[:], w[a, :, cc * 128:(cc + 1) * 128].rearrange("(kc p) c -> p kc c", p=128), writes=[ws])
                k.copy("dve" if i % 2 == 0 else "pool", wb, wb[:], ws, ws[:])
                ps = pss.next()
                for kc in range(16):
                    k.mm(ps, ps[:, 0:4], wb, wb[:, kc, :], cs, cs[:, kc, :], kc == 0, kc == 15)
                o = ot.next()
                k.ts("dve", o, o[:], ps, ps[:, 0:4], bt[:, a, cc:cc + 1], ALU.add, extra_reads=[bt])
                k.dma("sp", out[a, cc * 128:(cc + 1) * 128, :], o[:], reads=[o], is_output=True)
                i += 1
        k.finish()
    return nc


def run_mod(inputs):
    names = [("mix_ada_w", "mix_ada_b", 0), ("ffn_ada_w", "ffn_ada_b", 0), ("tm_ada_w", "tm_ada_b", 0), ("ffn_ada_w", "ffn_ada_b", 1)]
    nc = build_mod()
    cT = np.ascontiguousarray(inputs["c"].T)
    in_maps = []
    for j in range(8):
        sl = slice(j * MCOLS, (j + 1) * MCOLS)
        w = np.stack([np.ascontiguousarray(inputs[wn][i][:, sl]) for wn, bn, i in names])
        b = np.stack([inputs[bn][i][sl] for wn, bn, i in names])
        in_maps.append({"cT": cT, "w": w, "bia": b})
    res = run_bass_kernel_spmd(nc, in_maps, core_ids=list(range(8)))
    full = np.concatenate([r["out"] for r in res.results], axis=1)
    return np.ascontiguousarray(full.transpose(0, 2, 1))


NTOK_D = 2048
TT = 512


def to_pv(v):
    return np.ascontiguousarray(np.asarray(v, np.float32).reshape(16, 128).T)


def build_dense(final):
    nc = new_nc()
    xT = nc.dram_tensor("xT", [D, NTOK_D], F32, kind="ExternalInput").ap()
    ymT = nc.dram_tensor("ymT", [D, NTOK_D], BF16, kind="ExternalInput").ap()
    wo = nc.dram_tensor("wo", [D, D], F32, kind="ExternalInput").ap()
    w1 = nc.dram_tensor("w1", [D, DFF], F32, kind="ExternalInput").ap()
    w2 = nc.dram_tensor("w2", [DFF, D], F32, kind="ExternalInput").ap()
    pv = nc.dram_tensor("pv", [128, 6, 16], F32, kind="ExternalInput").ap()
    oT = nc.dram_tensor("oT", [D, NTOK_D], F32, kind="ExternalOutput").ap()
    with ExitStack() as es:
        k = KB(nc, es)
        pvt = k.sb([128, 6, 16], F32)
        k.dma("sp", pvt[:], pv, writes=[pvt])
        GM, NG, SC, SH, GF, FG = range(6)
        gs = k.sb([128, 16], F32)
        k.ts("dve", gs, gs[:], pvt, pvt[:, SC, :], 1.0, ALU.add)
        k.tt("dve", gs, gs[:], gs, gs[:], pvt, pvt[:, NG, :], ALU.mult)
        ones = k.sb([128, 128], BF16)
        k.memset("dve", ones, ones[:], 1.0)
        xt = k.sb([128, 16, TT], F32)
        ym = k.sb([128, 16, TT], BF16)
        hT = k.sb([128, 16, TT], BF16)
        aT = k.sb([128, 64, TT], BF16)
        wst = Rot([k.sb([128, 16, 128], F32) for _ in range(3)])
        wbf = Rot([k.sb([128, 16, 128], BF16) for _ in range(3)])
        sqr = Rot([k.sb([128, TT], BF16) for _ in range(2)])
        tmpr = Rot([k.sb([128, TT], F32) for _ in range(2)])
        rs = k.sb([128, TT], F32)
        pss = Rot([k.ps([128, TT]) for _ in range(4)])
        ssp = k.ps([128, TT])
        cast_i = [0]

        def wblock(w, r0, c0):
            ws = wst.next()
            wb = wbf.next()
            k.dma("sp", ws[:], w[r0:r0 + 2048, c0:c0 + 128].rearrange("(kc p) c -> p kc c", p=128), writes=[ws])
            e = "dve" if cast_i[0] % 2 == 0 else "pool"
            cast_i[0] += 1
            k.copy(e, wb, wb[:], ws, ws[:])
            return wb

        def rstd_of_x():
            for oc in range(16):
                sq = sqr.next()
                k.act(sq, sq[:], xt, xt[:, oc, :], AF.Square)
                k.mm(ssp, ssp[:], ones, ones[:], sq, sq[:], oc == 0, oc == 15)
            k.act(rs, rs[:], ssp, ssp[:], AF.Sqrt, bias=epsb[:, 0:1], scale=1.0 / D, extra_reads=[epsb])
            k.op("dve", lambda en: en.reciprocal(out=rs[:], in_=rs[:]), reads=[rs], writes=[rs])

        epsb = k.sb([128, 1], F32)
        k.memset("dve", epsb, epsb[:], EPS)
        for tt in range(NTOK_D // TT):
            tok = slice(tt * TT, (tt + 1) * TT)
            k.dma("sp", xt[:], xT[:, tok].rearrange("(kc p) t -> p kc t", p=128), writes=[xt])
            k.dma("sp", ym[:], ymT[:, tok].rearrange("(kc p) t -> p kc t", p=128), writes=[ym])
            for oc in range(16):
                wb = wblock(wo, 0, oc * 128)
                ps = pss.next()
                for kc in range(16):
                    k.mm(ps, ps[:], wb, wb[:, kc, :], ym, ym[:, kc, :], kc == 0, kc == 15)
                k.stt(xt, xt[:, oc, :], ps, ps[:], pvt[:, GM, oc:oc + 1], xt, xt[:, oc, :], ALU.mult, ALU.add, extra_reads=[pvt])
            rstd_of_x()
            for oc in range(16):
                tm = tmpr.next()
                k.stt(tm, tm[:], xt, xt[:, oc, :], gs[:, oc:oc + 1], rs, rs[:], ALU.mult, ALU.mult, extra_reads=[gs])
                k.act(hT, hT[:, oc, :], tm, tm[:], AF.Identity, bias=pvt[:, SH, oc:oc + 1], extra_reads=[pvt])
            for hc in range(64):
                wb = wblock(w1, 0, hc * 128)
                ps = pss.next()
                for kc in range(16):
                    k.mm(ps, ps[:], wb, wb[:, kc, :], hT, hT[:, kc, :], kc == 0, kc == 15)
                r = sqr.next()
                k.act(r, r[:], ps, ps[:], AF.Relu)
                k.tt("pool", aT, aT[:, hc, :], r, r[:], r, r[:], ALU.mult)
            for oc in range(16):
                ps = pss.next()
                for q in range(4):
                    wb = wblock(w2, q * 2048, oc * 128)
                    for kc in range(16):
                        k.mm(ps, ps[:], wb, wb[:, kc, :], aT, aT[:, q * 16 + kc, :], q == 0 and kc == 0, q == 3 and kc == 15)
                k.stt(xt, xt[:, oc, :], ps, ps[:], pvt[:, GF, oc:oc + 1], xt, xt[:, oc, :], ALU.mult, ALU.add, extra_reads=[pvt])
            if final:
                rstd_of_x()
                for oc in range(16):
                    k.stt(xt, xt[:, oc, :], xt, xt[:, oc, :], pvt[:, FG, oc:oc + 1], rs, rs[:], ALU.mult, ALU.mult, extra_reads=[pvt])
            k.dma("sp", oT[:, tok].rearrange("(kc p) t -> p kc t", p=128), xt[:], reads=[xt], is_output=True)
        k.finish()
    return nc


def run_dense(final, xT_full, ymT_full, wo, w1, w2, vecs):
    nc = build_dense(final)
    in_maps = []
    for j in range(8):
        b, s = j // 2, j % 2
        tok = slice(s * NTOK_D, (s + 1) * NTOK_D)
        pv = np.stack([to_pv(v) for v in vecs[b]], axis=1)
        in_maps.append({"xT": np.ascontiguousarray(xT_full[b][:, tok]), "ymT": np.ascontiguousarray(ymT_full[b][:, tok]),
                        "wo": wo, "w1": w1, "w2": w2, "pv": np.ascontiguousarray(pv)})
    res = run_bass_kernel_spmd(nc, in_maps, core_ids=list(range(8)))
    out = np.empty_like(xT_full)
    for j in range(8):
        b, s = j // 2, j % 2
        out[b][:, s * NTOK_D:(s + 1) * NTOK_D] = res.results[j]["oT"]
    return out


def kb_barrier(k):
    for e in k.eng:
        for f in k.eng:
            if f != e and k.cnt[f] > 0:
                k._wait(e, ("e", f, k.cnt[f]))
        for q, pool in k.dq.items():
            for slot, (sem, c) in enumerate(pool):
                if c > 0:
                    k._wait(e, ("d", q, slot, c))


KB.barrier = kb_barrier


class HProd:
    def __init__(self, k, TW, pvt, ig, isc, ish, ssp=None):
        self.k = k
        self.TW = TW
        self.pvt = pvt
        self.ish = ish
        self.gs = k.sb([128, 16], F32)
        k.ts("dve", self.gs, self.gs[:], pvt, pvt[:, isc, :], 1.0, ALU.add)
        k.tt("dve", self.gs, self.gs[:], self.gs, self.gs[:], pvt, pvt[:, ig, :], ALU.mult)
        self.ones = k.sb([128, 128], BF16)
        k.memset("dve", self.ones, self.ones[:], 1.0)
        self.epsb = k.sb([128, 1], F32)
        k.memset("dve", self.epsb, self.epsb[:], EPS)
        self.xt = k.sb([128, 16, TW], F32)
        self.hT = k.sb([128, 16, TW], BF16)
        self.sqr = Rot([k.sb([128, TW], BF16) for _ in range(2)])
        self.tmpr = Rot([k.sb([128, TW], F32) for _ in range(2)])
        self.rs = k.sb([128, TW], F32)
        self.ssp = ssp if ssp is not None else k.ps([128, 512])

    def make(self, xT_ap, c0=0, ncols=None):
        k, xt, hT, rs, ssp = self.k, self.xt, self.hT, self.rs, self.ssp
        ncols = self.TW if ncols is None else ncols
        k.dma("sp", xt[:, :, c0:c0 + ncols], xT_ap.rearrange("(kc p) t -> p kc t", p=128), writes=[xt])
        for oc in range(16):
            sq = self.sqr.next()
            k.act(sq, sq[:], xt, xt[:, oc, :], AF.Square)
            k.mm(ssp, ssp[:, 0:self.TW], self.ones, self.ones[:], sq, sq[:], oc == 0, oc == 15)
        k.act(rs, rs[:], ssp, ssp[:, 0:self.TW], AF.Sqrt, bias=self.epsb[:, 0:1], scale=1.0 / D, extra_reads=[self.epsb])
        k.op("dve", lambda en: en.reciprocal(out=rs[:], in_=rs[:]), reads=[rs], writes=[rs])
        for oc in range(16):
            tm = self.tmpr.next()
            k.stt(tm, tm[:], xt, xt[:, oc, :], self.gs[:, oc:oc + 1], rs, rs[:], ALU.mult, ALU.mult, extra_reads=[self.gs])
            k.act(hT, hT[:, oc, :], tm, tm[:], AF.Identity, bias=self.pvt[:, self.ish, oc:oc + 1], extra_reads=[self.pvt])
        return hT


GC = 128
NBLK = L // 128
GW = 128 + 128 + 256 + 256 + 32


def build_gla():
    nc = new_nc()
    xT = nc.dram_tensor("xT", [D, L], F32, kind="ExternalInput").ap()
    wg = nc.dram_tensor("wg", [2, D, GW], F32, kind="ExternalInput").ap()
    up = nc.dram_tensor("up", [2, 2, 16, 128], F32, kind="ExternalInput").ap()
    upb = nc.dram_tensor("upb", [128, 2, 2], F32, kind="ExternalInput").ap()
    gng = nc.dram_tensor("gng", [128, 2, 2], F32, kind="ExternalInput").ap()
    pv = nc.dram_tensor("pv", [128, 3, 16], F32, kind="ExternalInput").ap()
    msk = nc.dram_tensor("msk", [2, 128, 128], F32, kind="ExternalInput").ap()
    yT = nc.dram_tensor("yT", [2, 256, L], BF16, kind="ExternalOutput").ap()
    TW = 256
    with ExitStack() as es:
        k = KB(nc, es)
        pvt = k.sb([128, 3, 16], F32)
        k.dma("sp", pvt[:], pv, writes=[pvt])
        upbt = k.sb([128, 2, 2], F32)
        k.dma("sp", upbt[:], upb, writes=[upbt])
        nupb = k.sb([128, 2, 2], F32)
        k.ts("dve", nupb, nupb[:], upbt, upbt[:], -1.0, ALU.mult)
        gngt = k.sb([128, 2, 2], F32)
        k.dma("sp", gngt[:], gng, writes=[gngt])
        mk = k.sb([128, 2, 128], F32)
        k.dma("sp", mk[:], msk.rearrange("d s t -> s d t"), writes=[mk])
        ident = k.sb([128, 128], BF16)
        idf = k.sb([128, 128], F32)
        k.tt("dve", idf, idf[:], mk, mk[:, 0, :], mk, mk[:, 1, :], ALU.mult)
        k.copy("dve", ident, ident[:], idf, idf[:])
        ones = k.sb([128, 128], BF16)
        k.memset("dve", ones, ones[:], 1.0)
        epsb = k.sb([128, 1], F32)
        k.memset("dve", epsb, epsb[:], EPS)
        _scan_init = k.sb([128, 1], F32)
        k._ones_f = _scan_init
        k.memset("pool", _scan_init, _scan_init[:], 1.0)
        qT = k.sb([128, L], BF16)
        kT = k.sb([128, L], BF16)
        Vt = k.sb([128, NBLK, 256], BF16)
        rT = k.sb([128, 2, L], BF16)
        OT = k.sb([128, 2, L], F32)
        lrT = k.sb([16, 2, L], BF16)
        upt = k.sb([16, 2, 128], F32)
        upbf = k.sb([16, 2, 128], BF16)
        keT = Rot([k.sb([128, 128], BF16) for _ in range(2)])
        att = Rot([k.sb([128, 128], BF16) for _ in range(2)])
        S = k.sb([128, 256], F32)
        Sb = k.sb([128, 256], BF16)
        pss = Rot([k.ps([128, 512]) for _ in range(3)])
        psT = Rot([k.ps([128, 1024], BF16) for _ in range(1)])
        psS = k.ps([128, 512])
        sq2 = Rot([k.sb([128, 512], BF16) for _ in range(2)])
        rs2 = k.sb([128, 512], F32)
        ot = Rot([k.sb([128, 512], BF16) for _ in range(2)])
        sig = Rot([k.sb([128, 512], F32) for _ in range(2)])
        for hd in range(2):
            k.push()
            hp = HProd(k, TW, pvt, 0, 1, 2)
            wbf = k.sb([128, 16, GW], BF16)
            wst = Rot([k.sb([128, 16, 128], F32) for _ in range(2)])
            for c0 in range(0, GW, 128):
                cw = min(128, GW - c0)
                ws = wst.next()
                k.dma("sp", ws[:, :, 0:cw], wg[hd, :, c0:c0 + cw].rearrange("(kc p) c -> p kc c", p=128), writes=[ws])
                k.copy("pool", wbf, wbf[:, :, c0:c0 + cw], ws, ws[:, :, 0:cw])
            k.dma("sp", upt[:], up[hd].rearrange("d r c -> r d c"), writes=[upt])
            k.copy("dve", upbf, upbf[:], upt, upt[:])
            for tt in range(L // TW):
                t0 = tt * TW
                hT = hp.make(xT[:, t0:t0 + TW])
                for (c0, dst, sc) in ((0, qT, 128 ** -0.5), (128, kT, 1.0)):
                    ps = pss.next()
                    for kc in range(16):
                        k.mm(ps, ps[:, 0:TW], wbf, wbf[:, kc, c0:c0 + 128], hT, hT[:, kc, :], kc == 0, kc == 15)
                    k.act(dst, dst[:, t0:t0 + TW], ps, ps[:, 0:TW], AF.Copy, scale=sc)
                for vt in range(2):
                    ps = pss.next()
                    c0 = 512 + vt * 128
                    for kc in range(16):
                        k.mm(ps, ps[:, 0:TW], wbf, wbf[:, kc, c0:c0 + 128], hT, hT[:, kc, :], kc == 0, kc == 15)
                    k.copy("dve", rT, rT[:, vt, t0:t0 + TW], ps, ps[:, 0:TW])
                for d in range(2):
                    ps = pss.next()
                    c0 = 768 + 16 * d
                    for kc in range(16):
                        k.mm(ps, ps[0:16, 0:TW], wbf, wbf[:, kc, c0:c0 + 16], hT, hT[:, kc, :], kc == 0, kc == 15)
                    k.copy("dve", lrT, lrT[:, d, t0:t0 + TW], ps, ps[0:16, 0:TW])
                for tb in range(TW // 128):
                    ps = pss.next()
                    for kc in range(16):
                        k.mm(ps, ps[:, 0:256], hT, hT[:, kc, tb * 128:(tb + 1) * 128], wbf, wbf[:, kc, 256:512], kc == 0, kc == 15)
                    k.act(Vt, Vt[:, tt * (TW // 128) + tb, :], ps, ps[:, 0:256], AF.Copy)
            k.pop()
            k.push()
            gT = k.sb([128, L], F32)
            cum = k.sb([128, L], F32)
            ex = k.sb([128, L], F32)
            qs = k.sb([128, L], BF16)
            ks = k.sb([128, L], BF16)
            ke = k.sb([128, L], BF16)
            for d in range(2):
                for t0 in range(0, L, 512):
                    ps = pss.next()
                    k.mm(ps, ps[:], upbf, upbf[:, d, :], lrT, lrT[:, d, t0:t0 + 512], True, True)
                    k.act(gT, gT[:, t0:t0 + 512], ps, ps[:], AF.Exp, bias=nupb[:, hd, d:d + 1], scale=-1.0, extra_reads=[nupb])
                k.act(gT, gT[:], gT, gT[:], AF.Ln, bias=1.0)
                k.ts("dve", gT, gT[:], gT, gT[:], -1.0 / 16.0, ALU.mult)
                _scan(k, cum, gT)
                cv = cum[:].rearrange("p (n c) -> p n c", c=GC)
                gv = gT[:].rearrange("p (n c) -> p n c", c=GC)
                exv = ex[:].rearrange("p (n c) -> p n c", c=GC)
                if d == 0:
                    base = k.sb([128, NBLK], F32, name=f"base{hd}{d}")
                    k.tt("dve", base, base[:], cum, cv[:, :, 0], gT, gv[:, :, 0], ALU.subtract)
                    k.tt("dve", ex, exv, cum, cv, base, base[:].unsqueeze(2).to_broadcast([128, NBLK, GC]), ALU.subtract)
                    last = GC - 1
                else:
                    base = k.sb([128, NBLK], F32, name=f"base{hd}{d}")
                    k.copy("dve", base, base[:], cum, cv[:, :, GC - 1])
                    k.tt("dve", ex, exv, cum, cv, base, base[:].unsqueeze(2).to_broadcast([128, NBLK, GC]), ALU.subtract)
                    k.ts("dve", ex, ex[:], ex, ex[:], -1.0, ALU.mult)
                    k.tt("dve", ex, ex[:], ex, ex[:], gT, gT[:], ALU.add)
                    last = 0
                glast = k.sb([128, NBLK], F32, name=f"glast{hd}{d}")
                k.copy("dve", glast, glast[:], ex, exv[:, :, last])
                k.tt("dve", cum, cv, ex, exv, glast, glast[:].unsqueeze(2).to_broadcast([128, NBLK, GC]), ALU.subtract)
                k.act(glast, glast[:], glast, glast[:], AF.Exp)
                k.act(gT, gT[:], ex, ex[:], AF.Exp)
                k.tt("dve", qs, qs[:], qT, qT[:], gT, gT[:], ALU.mult)
                k.act(gT, gT[:], ex, ex[:], AF.Exp, scale=-1.0)
                k.tt("dve", ks, ks[:], kT, kT[:], gT, gT[:], ALU.mult)
                k.act(gT, gT[:], cum, cum[:], AF.Exp, scale=-1.0)
                k.tt("dve", ke, ke[:], kT, kT[:], gT, gT[:], ALU.mult)
                k.memset("dve", S, S[:], 0.0)
                k.memset("pool", Sb, Sb[:], 0.0)
                order = range(NBLK) if d == 0 else range(NBLK - 1, -1, -1)
                for n in order:
                    tk = slice(n * 128, (n + 1) * 128)
                    pa = pss.next()
                    k.mm(pa, pa[:, 0:128], ks, ks[:, tk], qs, qs[:, tk], True, True)
                    at = att.next()
                    k.tt("dve", at, at[:], pa, pa[:, 0:128], mk, mk[:, d, :], ALU.mult)
                    pt = psT.next()
                    k.transpose(pt, pt[:, 0:128], ke, ke[:, tk], ident, ident[:])
                    kt = keT.next()
                    k.copy("act", kt, kt[:], pt, pt[:, 0:128])
                    po = pss.next()
                    for vt in range(2):
                        k.mm(po, po[:, vt * 128:(vt + 1) * 128], Sb, Sb[:, vt * 128:(vt + 1) * 128], qs, qs[:, tk], True, False)
                        k.mm(po, po[:, vt * 128:(vt + 1) * 128], Vt, Vt[:, n, vt * 128:(vt + 1) * 128], at, at[:], False, True)
                    pov = po[:, 0:256].rearrange("p (v t) -> p v t", v=2)
                    if d == 0:
                        k.copy("act", OT, OT[:, :, tk], po, pov)
                    else:
                        k.tt("dve", OT, OT[:, :, tk], po, pov, OT, OT[:, :, tk], ALU.add)
                    k.mm(psS, psS[:, 0:256], kt, kt[:], Vt, Vt[:, n, :], True, True)
                    k.stt(S, S[:], S, S[:], glast[:, n:n + 1], psS, psS[:, 0:256], ALU.mult, ALU.add, extra_reads=[glast])
                    k.copy("act", Sb, Sb[:], S, S[:])
            k.pop()
            for t0 in range(0, L, 512):
                ps = pss.next()
                for vt in range(2):
                    sq = sq2.next()
                    k.act(sq, sq[:], OT, OT[:, vt, t0:t0 + 512], AF.Square)
                    k.mm(ps, ps[:], ones, ones[:], sq, sq[:], vt == 0, vt == 1)
                k.act(rs2, rs2[:], ps, ps[:], AF.Sqrt, bias=epsb[:, 0:1], scale=1.0 / 256, extra_reads=[epsb])
                k.op("dve", lambda en: en.reciprocal(out=rs2[:], in_=rs2[:]), reads=[rs2], writes=[rs2])
                for vt in range(2):
                    sg = sig.next()
                    k.act(sg, sg[:], rT, rT[:, vt, t0:t0 + 512], AF.Sigmoid)
                    k.tt("pool", sg, sg[:], sg, sg[:], rT, rT[:, vt, t0:t0 + 512], ALU.mult)
                    tm = sig.next()
                    k.stt(tm, tm[:], OT, OT[:, vt, t0:t0 + 512], gngt[:, hd, vt:vt + 1], rs2, rs2[:], ALU.mult, ALU.mult, extra_reads=[gngt])
                    o = ot.next()
                    k.tt("dve", o, o[:], tm, tm[:], sg, sg[:], ALU.mult)
                    k.dma("sp", yT[hd, vt * 128:(vt + 1) * 128, t0:t0 + 512], o[:], reads=[o], is_output=True)
        k.finish()
    return nc


def _scan(k, out_b, in_b):
    if not hasattr(k, "_ones_f"):
        k._ones_f = k.sb([128, 1], F32)
        k.memset("pool", k._ones_f, k._ones_f[:], 1.0)
    of = k._ones_f
    n = in_b.t.shape[-1]
    k.op("dve", lambda en: en.tensor_tensor_scan(out=out_b[:], data0=of[:, 0:1].to_broadcast([128, n]), data1=in_b[:], initial=0.0, op0=ALU.mult, op1=ALU.add),
         reads=[of, in_b], writes=[out_b])


def gla_masks():
    s = np.arange(128)[:, None]
    t = np.arange(128)[None, :]
    return np.stack([(s <= t), (s >= t)]).astype(np.float32)


def run_gla(xT_full, inputs, mod):
    nc = build_gla()
    w_in = inputs["ab_w_in"][0]
    in_maps = []
    for j in range(8):
        b, s = j // 2, j % 2
        wl = []
        for hh in (2 * s, 2 * s + 1):
            cols = np.concatenate([np.arange(3072 + 128 * hh, 3072 + 128 * hh + 128), np.arange(3584 + 128 * hh, 3584 + 128 * hh + 128),
                                   np.arange(4096 + 256 * hh, 4096 + 256 * hh + 256), np.arange(5120 + 256 * hh, 5120 + 256 * hh + 256),
                                   np.arange(6144, 6176)])
            wl.append(w_in[:, cols])
        up = np.stack([inputs["gla_up"][0][:, :, 128 * hh:128 * hh + 128] for hh in (2 * s, 2 * s + 1)])
        upb = np.stack([inputs["gla_up_b"][0][:, 128 * hh:128 * hh + 128] for hh in (2 * s, 2 * s + 1)])
        gng = np.stack([inputs["gla_norm_g"][0][256 * hh:256 * hh + 256].reshape(2, 128) for hh in (2 * s, 2 * s + 1)])
        pv = np.stack([to_pv(inputs["mix_norm_g"][0]), to_pv(mod[b, 2048:4096]), to_pv(mod[b, 0:2048])], axis=1)
        in_maps.append({"xT": xT_full[b], "wg": np.ascontiguousarray(np.stack(wl)), "up": np.ascontiguousarray(up),
                        "upb": np.ascontiguousarray(upb.transpose(2, 0, 1)), "gng": np.ascontiguousarray(gng.transpose(2, 0, 1)),
                        "pv": np.ascontiguousarray(pv), "msk": gla_masks()})
    res = run_bass_kernel_spmd(nc, in_maps, core_ids=list(range(8)))
    out = np.empty((4, 1024, L), NPBF)
    for j in range(8):
        b, s = j // 2, j % 2
        out[b, 512 * s:512 * s + 512] = res.results[j]["yT"].reshape(512, L)
    return out


KROW = 8192 + 256


def hyena_consts():
    Lf = np.float32(L)
    t = np.linspace(0.0, 1.0, L, dtype=np.float32)
    ang = (np.float32(2.0 * np.pi) / Lf) * np.arange(L, dtype=np.float32)
    bands = np.linspace(1e-4, 15.0, 16, dtype=np.float32)
    z = np.concatenate([t[:, None], np.cos(bands[None, :] * ang[:, None]), -np.sin(bands[None, :] * ang[:, None])], axis=-1)
    zr = np.zeros_like(z)
    zr[1:] = z[:0:-1]
    tr = np.full(L, 1e4, np.float32)
    tr[1:] = t[:0:-1]
    zz = np.stack([z.T, zr.T]).astype(np.float32)
    zh = zz.astype(NPBF)
    zl = (zz - zh.astype(np.float32)).astype(NPBF)
    tt = np.stack([np.broadcast_to(t, (128, L)), np.broadcast_to(tr, (128, L))]).astype(np.float32)
    return np.ascontiguousarray(zh), np.ascontiguousarray(zl), np.ascontiguousarray(tt)


def build_hyena():
    nc = new_nc()
    xT = nc.dram_tensor("xT", [4, D, L], F32, kind="ExternalInput").ap()
    wh = nc.dram_tensor("wh", [D, 384], F32, kind="ExternalInput").ap()
    pvb = nc.dram_tensor("pvb", [128, 4, 3, 16], F32, kind="ExternalInput").ap()
    cw = nc.dram_tensor("cw", [128, 3, 4], F32, kind="ExternalInput").ap()
    fw1 = nc.dram_tensor("fw1", [33, 64], F32, kind="ExternalInput").ap()
    fw2 = nc.dram_tensor("fw2", [64, 64], F32, kind="ExternalInput").ap()
    fvec = nc.dram_tensor("fvec", [64, 4], F32, kind="ExternalInput").ap()
    fw3 = nc.dram_tensor("fw3", [64, 4, 128], F32, kind="ExternalInput").ap()
    dec = nc.dram_tensor("dec", [128, 4], F32, kind="ExternalInput").ap()
    skp = nc.dram_tensor("skp", [128, 2], F32, kind="ExternalInput").ap()
    zh = nc.dram_tensor("zh", [2, 33, L], BF16, kind="ExternalInput").ap()
    zl = nc.dram_tensor("zl", [2, 33, L], BF16, kind="ExternalInput").ap()
    tts = nc.dram_tensor("tts", [2, 128, L], F32, kind="ExternalInput").ap()
    idn = nc.dram_tensor("idn", [128, 128], F32, kind="ExternalInput").ap()
    yT = nc.dram_tensor("yT", [4, 128, L], BF16, kind="ExternalOutput").ap()
    Kd_h = nc.dram_tensor("Kd", [2, 128, KROW], BF16, kind="Internal")
    Kd = Kd_h.ap()
    Xs = nc.dram_tensor("Xs", [4, 2, 128, L], BF16, kind="Internal").ap()
    TW = 256
    HALF = np.float32(np.pi / 2)
    with ExitStack() as es:
        k = KB(nc, es)
        KdB = Buf(None)
        XsB = Buf(None)
        idf = k.sb([128, 128], F32)
        k.dma("sp", idf[:], idn, writes=[idf])
        ident = k.sb([128, 128], BF16)
        k.copy("dve", ident, ident[:], idf, idf[:])
        cwt = k.sb([128, 3, 4], F32)
        k.dma("sp", cwt[:], cw, writes=[cwt])
        pvt = k.sb([128, 4, 3, 16], F32)
        k.dma("sp", pvt[:], pvb, writes=[pvt])
        Vt = k.sb([128, 128, 4, 32], BF16)
        Yt = k.sb([128, 128, 4, 32], BF16)
        pss = Rot([k.ps([128, 512]) for _ in range(4)])
        psT = Rot([k.ps([128, 1024], BF16) for _ in range(2)])
        k.push()
        w1f = k.sb([33, 64], F32)
        k.dma("sp", w1f[:], fw1, writes=[w1f])
        w1h = k.sb([33, 64], BF16)
        w1l = k.sb([33, 64], BF16)
        w1r = k.sb([33, 64], F32)
        k.copy("dve", w1h, w1h[:], w1f, w1f[:])
        k.tt("dve", w1r, w1r[:], w1f, w1f[:], w1h, w1h[:], ALU.subtract)
        k.copy("dve", w1l, w1l[:], w1r, w1r[:])
        w2f = k.sb([64, 64], F32)
        k.dma("sp", w2f[:], fw2, writes=[w2f])
        w2b = k.sb([64, 64], BF16)
        k.copy("dve", w2b, w2b[:], w2f, w2f[:])
        w3f = k.sb([64, 4, 128], F32)
        k.dma("sp", w3f[:], fw3, writes=[w3f])
        w3b = k.sb([64, 4, 128], BF16)
        k.copy("dve", w3b, w3b[:], w3f, w3f[:])
        fv = k.sb([64, 4], F32)
        k.dma("sp", fv[:], fvec, writes=[fv])
        sc = k.sb([64, 2, 3], F32)
        for ly in range(2):
            k.ts("dve", sc, sc[:, ly, 0:1], fv, fv[:, 2 * ly + 1:2 * ly + 2], 0.25, ALU.mult)
            k.tt("dve", sc, sc[:, ly, 1:2], sc, sc[:, ly, 0:1], fv, fv[:, 2 * ly:2 * ly + 1], ALU.mult)
            k.ts("dve", sc, sc[:, ly, 2:3], sc, sc[:, ly, 1:2], float(HALF), ALU.add)
        dct = k.sb([128, 4], F32)
        k.dma("sp", dct[:], dec, writes=[dct])
        k.act(dct, dct[:], dct, dct[:], AF.Abs)
        k.ts("dve", dct, dct[:], dct, dct[:], -1.0, ALU.mult)
        skt = k.sb([128, 2], F32)
        k.dma("sp", skt[:], skp, writes=[skt])
        zht = k.sb([33, 2, L], BF16)
        zlt = k.sb([33, 2, L], BF16)
        k.dma("sp", zht[:], zh.rearrange("d f t -> f d t"), writes=[zht])
        k.dma("sp", zlt[:], zl.rearrange("d f t -> f d t"), writes=[zlt])
        f1 = k.sb([64, 2, L], BF16)
        f2 = k.sb([64, 2, L], BF16)
        s1 = Rot([k.sb([64, 512], F32) for _ in range(2)])
        c1 = Rot([k.sb([64, 512], F32) for _ in range(2)])

        def sin_layer(ps, ly, dst_ap, dst_b):
            a = s1.next()
            b = c1.next()
            k.act(a, a[:], ps, ps[0:64, :], AF.Sin, bias=sc[:, ly, 1:2], scale=sc[:, ly, 0:1], extra_reads=[sc])
            k.act(b, b[:], ps, ps[0:64, :], AF.Sin, bias=sc[:, ly, 2:3], scale=sc[:, ly, 0:1], extra_reads=[sc])
            k.tt("dve", b, b[:], a, a[:], b, b[:], ALU.mult)
            k.tt("dve", a, a[:], a, a[:], a, a[:], ALU.mult)
            k.ts("dve", a, a[:], a, a[:], -8.0, ALU.mult, 4.0, ALU.add)
            k.tt("dve", dst_b, dst_ap, a, a[:], b, b[:], ALU.mult)

        for d in range(2):
            for t0 in range(0, L, 512):
                ps = pss.next()
                k.mm(ps, ps[0:64, :], w1h, w1h[:], zht, zht[:, d, t0:t0 + 512], True, False)
                k.mm(ps, ps[0:64, :], w1l, w1l[:], zht, zht[:, d, t0:t0 + 512], False, False)
                k.mm(ps, ps[0:64, :], w1h, w1h[:], zlt, zlt[:, d, t0:t0 + 512], False, True)
                sin_layer(ps, 0, f1[:, d, t0:t0 + 512], f1)
        for d in range(2):
            for t0 in range(0, L, 512):
                ps = pss.next()
                k.mm(ps, ps[0:64, :], w2b, w2b[:], f1, f1[:, d, t0:t0 + 512], True, True)
                sin_layer(ps, 1, f2[:, d, t0:t0 + 512], f2)
        tt_t = k.sb([128, 2, L], F32)
        k.dma("sp", tt_t[:], tts.rearrange("d p t -> p d t"), writes=[tt_t])
        krow = k.sb([128, KROW], BF16)
        win = Rot([k.sb([128, 512], F32) for _ in range(2)])
        for o in range(2):
            k.memset("pool", krow, krow[:], 0.0)
            for d in range(2):
                od = o * 2 + d
                fi = 1 - d
                p0 = 128 + (0 if d == 0 else 4096)
                for t0 in range(0, L, 512):
                    ps = pss.next()
                    k.mm(ps, ps[:], w3b, w3b[:, od, :], f2, f2[:, fi, t0:t0 + 512], True, True)
                    wn = win.next()
                    k.act(wn, wn[:], tt_t, tt_t[:, fi, t0:t0 + 512], AF.Exp, scale=dct[:, od:od + 1], extra_reads=[dct])
                    k.tt("dve", krow, krow[:, p0 + t0:p0 + t0 + 512], ps, ps[:], wn, wn[:], ALU.mult)
            ps = pss.next()
            k.mm(ps, ps[:], w3b, w3b[:, o * 2, :], f2, f2[:, 0, 0:512], True, True)
            wn = win.next()
            k.act(wn, wn[:], tt_t, tt_t[:, 0, 0:512], AF.Exp, scale=dct[:, o * 2:o * 2 + 1], extra_reads=[dct])
            tmpk = k.sb([128, 512], F32, name=f"tmpk{o}")
            k.tt("dve", tmpk, tmpk[:], ps, ps[:], wn, wn[:], ALU.mult)
            k.ts("dve", krow, krow[:, 128 + 4096:128 + 4097], tmpk, tmpk[:, 0:1], skt[:, o:o + 1], ALU.add, extra_reads=[skt])
            k.dma("sp", Kd[o], krow[:], reads=[krow], writes=[KdB])
        k.pop()
        k.push()
        wst = Rot([k.sb([128, 16, 128], F32) for _ in range(2)])
        wbf = k.sb([128, 16, 384], BF16)
        for c0 in range(0, 384, 128):
            ws = wst.next()
            k.dma("sp", ws[:], wh[:, c0:c0 + 128].rearrange("(kc p) c -> p kc c", p=128), writes=[ws])
            k.copy("pool", wbf, wbf[:, :, c0:c0 + 128], ws, ws[:])
        P = k.sb([128, 3, L + 2], BF16)
        U = k.sb([128, 3, L], BF16)
        k.memset("pool", P, P[:], 0.0)
        uf = k.sb([128, L], F32)
        for b in range(4):
            pv_b = k.sb([128, 3, 16], F32, name=f"pvb{b}")
            k.copy("dve", pv_b, pv_b[:], pvt, pvt[:, b, :, :])
            k.push()
            hp = HProd(k, TW, pv_b, 0, 1, 2)
            for tt_ in range(L // TW):
                t0 = tt_ * TW
                hT = hp.make(xT[b, :, t0:t0 + TW])
                for part in range(3):
                    ps = pss.next()
                    for kc in range(16):
                        k.mm(ps, ps[:, 0:TW], wbf, wbf[:, kc, part * 128:(part + 1) * 128], hT, hT[:, kc, :], kc == 0, kc == 15)
                    k.copy("act", P, P[:, part, 1 + t0:1 + t0 + TW], ps, ps[:, 0:TW])
            k.pop()
            for part in range(3):
                k.ts("dve", uf, uf[:], P, P[:, part, 1:L + 1], cwt[:, part, 1:2], ALU.mult, cwt[:, part, 3:4], ALU.add, extra_reads=[cwt])
                k.stt(uf, uf[:], P, P[:, part, 0:L], cwt[:, part, 0:1], uf, uf[:], ALU.mult, ALU.add, extra_reads=[cwt])
                k.stt(U, U[:, part, :], P, P[:, part, 2:L + 2], cwt[:, part, 2:3], uf, uf[:], ALU.mult, ALU.add, extra_reads=[cwt])
            for J in range(32):
                pt = psT.next()
                k.transpose(pt, pt[:, 0:128], U, U[:, 0, J * 128:(J + 1) * 128], ident, ident[:])
                k.copy("act" if J % 2 else "dve", Vt, Vt[:, :, b, J], pt, pt[:, 0:128])
            k.dma("sp", Xs[b, 0], U[:, 1, :], reads=[U], writes=[XsB])
            k.dma("sp", Xs[b, 1], U[:, 2, :], reads=[U], writes=[XsB])
        k.pop()
        k.push()
        ksh = Rot([k.sb([128, 8320], BF16) for _ in range(2)])
        xg = k.sb([128, L], BF16)
        yfm = k.sb([128, L], BF16)
        zf = k.sb([128, L], BF16)
        for o in range(2):
            for c in range(128):
                ks_ = ksh.next()
                src = bass.AP(Kd_h, (o * 128 + c) * KROW, [[1, 128], [1, 8320]])
                k.dma("sp", ks_[:], src, reads=[KdB], writes=[ks_])
                ps = pss.next()
                psv = ps[:, 0:128].rearrange("p (b i) -> p b i", b=4)
                ds = [0] + [x for dd in range(1, 32) for x in (dd, -dd)]
                for n_, dlag in enumerate(ds):
                    J0, J1 = max(0, -dlag), min(32, 32 - dlag)
                    k.mm(ps, psv[:, :, J0 + dlag:J1 + dlag], ks_, ks_[:, 4097 - 128 * dlag:4097 - 128 * dlag + 128],
                         Vt, Vt[:, c, :, J0:J1], n_ == 0, n_ == len(ds) - 1)
                k.copy("act" if c % 2 else "dve", Yt, Yt[:, c, :, :], ps, psv)
            for b in range(4):
                for I in range(32):
                    pt = psT.next()
                    k.transpose(pt, pt[:, 0:128], Yt, Yt[:, :, b, I], ident, ident[:])
                    k.copy("act" if I % 2 else "dve", yfm, yfm[:, I * 128:(I + 1) * 128][:, ::-1], pt, pt[:, 0:128])
                k.dma("sp", xg[:], Xs[b, o], reads=[XsB], writes=[xg])
                if o == 0:
                    k.tt("pool", zf, zf[:], xg, xg[:], yfm, yfm[:], ALU.mult)
                    for J in range(32):
                        pt = psT.next()
                        k.transpose(pt, pt[:, 0:128], zf, zf[:, J * 128:(J + 1) * 128], ident, ident[:])
                        k.copy("act" if J % 2 else "dve", Vt, Vt[:, :, b, J], pt, pt[:, 0:128])
                else:
                    k.tt("pool", zf, zf[:], xg, xg[:], yfm, yfm[:], ALU.mult)
                    k.dma("sp", yT[b], zf[:], reads=[zf], is_output=True)
        k.pop()
        k.finish()
    return nc


def run_hyena(xT_full, inputs, mod):
    nc = build_hyena()
    zh, zl, tts = hyena_consts()
    w_in = inputs["ab_w_in"][0]
    cwf = inputs["hy_conv_w"][0]
    cbf = inputs["hy_conv_b"][0]
    w3 = inputs["hy_ffn_w3"][0].reshape(64, 2, 2, 1024)
    dec = inputs["hy_decay"][0].reshape(2, 2, 1024)
    skp = inputs["hy_skip"][0]
    fvec = np.stack([inputs["hy_ffn_b1"][0], inputs["hy_freq1"][0], inputs["hy_ffn_b2"][0], inputs["hy_freq2"][0]], axis=1)
    pvb = np.stack([np.stack([to_pv(inputs["mix_norm_g"][0]), to_pv(mod[b, 2048:4096]), to_pv(mod[b, 0:2048])], axis=1)
                    for b in range(4)], axis=1)
    in_maps = []
    for j in range(8):
        ch = np.arange(128 * j, 128 * j + 128)
        cols = np.concatenate([ch, 1024 + ch, 2048 + ch])
        cw = np.stack([np.concatenate([cwf[:, p * 1024 + ch].T, cbf[p * 1024 + ch][:, None]], axis=1) for p in range(3)], axis=1)
        in_maps.append({
            "xT": xT_full, "wh": np.ascontiguousarray(w_in[:, cols]), "pvb": np.ascontiguousarray(pvb),
            "cw": np.ascontiguousarray(cw.astype(np.float32)), "fw1": inputs["hy_ffn_w1"][0], "fw2": inputs["hy_ffn_w2"][0],
            "fvec": np.ascontiguousarray(fvec.astype(np.float32)),
            "fw3": np.ascontiguousarray(w3[:, :, :, ch].reshape(64, 4, 128)),
            "dec": np.ascontiguousarray(dec[:, :, ch].reshape(4, 128).T), "skp": np.ascontiguousarray(skp[:, ch].T),
            "zh": zh, "zl": zl, "tts": tts, "idn": np.eye(128, dtype=np.float32)})
    res = run_bass_kernel_spmd(nc, in_maps, core_ids=list(range(8)))
    out = np.empty((4, 1024, L), NPBF)
    for j in range(8):
        out[:, 128 * j:128 * j + 128, :] = res.results[j]["yT"]
    return out


RW_DECAY = 0.606531
GN_EPS = 64e-5
RSEG = 1024


def rwkv_masks():
    s = np.arange(128)[:, None]
    t = np.arange(128)[None, :]
    same = (s // 64) == (t // 64)
    out = []
    for d in range(2):
        if d == 0:
            ms, mi = same & (s < t), same & (s <= t)
        else:
            ms, mi = same & (s > t), same & (s >= t)
        out.append(np.concatenate([ms, mi, ms, mi, ms.T], axis=1))
    return np.stack(out).astype(np.float32)


def build_rwkv(stop=0):
    nc = new_nc()
    xT = nc.dram_tensor("xT", [D, L], F32, kind="ExternalInput").ap()
    pv = nc.dram_tensor("pv", [128, 3, 16], F32, kind="ExternalInput").ap()
    mu = nc.dram_tensor("mu", [128, 6, 16], F32, kind="ExternalInput").ap()
    wrkv = nc.dram_tensor("wrkv", [3, D, 1024], F32, kind="ExternalInput").ap()
    w1 = nc.dram_tensor("w1", [D, 2, 96], F32, kind="ExternalInput").ap()
    a1 = nc.dram_tensor("a1", [D, 2, 96], F32, kind="ExternalInput").ap()
    g1 = nc.dram_tensor("g1", [D, 256], F32, kind="ExternalInput").ap()
    w2 = nc.dram_tensor("w2", [96, 2, 1024], F32, kind="ExternalInput").ap()
    a2 = nc.dram_tensor("a2", [96, 2, 1024], F32, kind="ExternalInput").ap()
    g2 = nc.dram_tensor("g2", [256, 1024], F32, kind="ExternalInput").ap()
    pc = nc.dram_tensor("pc", [128, 9, 8], F32, kind="ExternalInput").ap()
    msk = nc.dram_tensor("msk", [2, 128, 640], F32, kind="ExternalInput").ap()
    cst = nc.dram_tensor("cst", [128, 258], F32, kind="ExternalInput").ap()
    yT = nc.dram_tensor("yT", [1024, L], BF16, kind="ExternalOutput").ap()
    Rd = nc.dram_tensor("Rd", [1024, L], BF16, kind="Internal").ap()
    Kdd = nc.dram_tensor("Kdd", [1024, L], BF16, kind="Internal").ap()
    VTd = nc.dram_tensor("VTd", [1024, L], BF16, kind="Internal").ap()
    Gd = nc.dram_tensor("Gd", [1024, L], BF16, kind="Internal").ap()
    Ad = nc.dram_tensor("Ad", [2, 1024, L], BF16, kind="Internal").ap()
    LWd = nc.dram_tensor("LWd", [2, 1024, L], F32, kind="Internal").ap()
    Vd = nc.dram_tensor("Vd", [L, 1024], BF16, kind="Internal").ap()
    TW = 256
    TH = TW + 2
    with ExitStack() as es:
        k = KB(nc, es)
        DR = {n: Buf(None) for n in ("R", "K", "VT", "G", "A", "LW", "V")}
        pvt = k.sb([128, 3, 16], F32)
        k.dma("sp", pvt[:], pv, writes=[pvt])
        mut = k.sb([128, 6, 16], F32)
        k.dma("sp", mut[:], mu, writes=[mut])
        pct = k.sb([128, 9, 8], F32)
        k.dma("sp", pct[:], pc, writes=[pct])
        cs = k.sb([128, 258], F32)
        k.dma("sp", cs[:], cst, writes=[cs])
        bo1 = k.sb([128, 128], BF16)
        k.copy("dve", bo1, bo1[:], cs, cs[:, 0:128])
        bo64 = k.sb([128, 128], BF16)
        k.ts("dve", bo64, bo64[:], cs, cs[:, 0:128], 1.0 / 64.0, ALU.mult)
        ident = k.sb([128, 128], BF16)
        k.copy("dve", ident, ident[:], cs, cs[:, 128:256])
        hm = cs
        banks = [k.ps([128, 512]) for _ in range(7)]
        bankT = k.ps([128, 1024], BF16)
        pss = Rot(banks[0:4])
        k.push()
        hp = HProd(k, TH, pvt, 0, 1, 2, ssp=banks[6])
        k.memset("pool", hp.xt, hp.xt[:], 0.0)
        lw1 = k.sb([128, 16, 2, 96], BF16)
        la1 = k.sb([128, 16, 2, 96], BF16)
        lg1 = k.sb([128, 16, 256], BF16)
        lst = k.sb([128, 16, 256], F32)
        for (dst, srcw, wd) in ((lw1, w1, 192), (la1, a1, 192), (lg1, g1, 256)):
            sv = srcw.rearrange("(kc p) d c -> p kc (d c)", p=128) if wd == 192 else srcw.rearrange("(kc p) c -> p kc c", p=128)
            k.dma("sp", lst[:, :, 0:wd], sv, writes=[lst])
            dv = dst[:].rearrange("p kc d c -> p kc (d c)") if wd == 192 else dst[:]
            k.copy("dve", dst, dv, lst, lst[:, :, 0:wd])
        lw2 = k.sb([96, 2, 1024], BF16)
        la2 = k.sb([96, 2, 1024], BF16)
        lg2 = k.sb([128, 2, 1024], BF16)
        l2s = k.sb([128, 2, 1024], F32)
        k.dma("sp", l2s[0:96], w2, writes=[l2s])
        k.copy("dve", lw2, lw2[:], l2s, l2s[0:96])
        k.dma("sp", l2s[0:96], a2, writes=[l2s])
        k.copy("dve", la2, la2[:], l2s, l2s[0:96])
        k.dma("sp", l2s[:], g2.rearrange("(kc p) c -> p kc c", p=128), writes=[l2s])
        k.copy("dve", lg2, lg2[:], l2s, l2s[:])
        sT = k.sb([128, 16, TW], F32)
        xx = k.sb([128, 16, TW], BF16)
        xjr = Rot([k.sb([128, 16, TW], BF16) for _ in range(2)])
        wst = Rot([k.sb([128, 16, 128], F32) for _ in range(2)])
        wbf = Rot([k.sb([128, 16, 128], BF16) for _ in range(2)])
        stg = Rot([k.sb([128, 8, TW], BF16) for _ in range(3)])
        stgf = Rot([k.sb([128, 8, TW], F32) for _ in range(2)])
        vtok = k.sb([128, 2, 1024], BF16)
        th = Rot([k.sb([128, 2, TW], BF16) for _ in range(2)])
        sgm = Rot([k.sb([128, TW], F32) for _ in range(2)])
        ci = [0]

        def wblock(w_ap):
            ws = wst.next()
            wb = wbf.next()
            k.dma("sp", ws[:], w_ap.rearrange("(kc p) c -> p kc c", p=128), writes=[ws])
            k.copy("pool" if ci[0] % 2 else "dve", wb, wb[:], ws, ws[:])
            ci[0] += 1
            return wb

        def mix(j, hT):
            xj = xjr.next()
            for oc in range(16):
                k.stt(xj, xj[:, oc, :], xx, xx[:, oc, :], mut[:, j, oc:oc + 1], hT, hT[:, oc, 1:TW + 1], ALU.mult, ALU.add, extra_reads=[mut])
            return xj

        for tt_ in range(L // TW):
            t0 = tt_ * TW
            if tt_ == 0:
                hT = hp.make(xT[:, 0:TW + 1], 1, TW + 1)
                k.memset("pool", hT, hT[:, :, 0:1], 0.0)
            elif tt_ == L // TW - 1:
                hT = hp.make(xT[:, t0 - 1:L], 0, TW + 1)
                k.memset("pool", hT, hT[:, :, TW + 1:TW + 2], 0.0)
            else:
                hT = hp.make(xT[:, t0 - 1:t0 + TW + 1], 0, TH)
            k.tt("pool", sT, sT[:], hT, hT[:, :, 0:TW], hT, hT[:, :, 2:TW + 2], ALU.add)
            k.stt(xx, xx[:], sT, sT[:], 0.5, hT, hT[:, :, 1:TW + 1], ALU.mult, ALU.subtract)
            dst_sl = lambda ap: ap.rearrange("(c p) t -> p c t", p=128)[:, :, t0:t0 + TW]
            for (j, wi, dname, dap) in ((0, 0, "R", Rd), (2, 1, "K", Kdd), (3, 2, "VT", VTd)):
                xj = mix(j, hT)
                sg_ = stg.next()
                for c in range(8):
                    wb = wblock(wrkv[wi, :, c * 128:(c + 1) * 128])
                    ps = pss.next()
                    for kc in range(16):
                        k.mm(ps, ps[:, 0:TW], wb, wb[:, kc, :], xj, xj[:, kc, :], kc == 0, kc == 15)
                    k.copy("act", sg_, sg_[:, c, :], ps, ps[:, 0:TW])
                    if j == 3:
                        for tb in range(2):
                            ps2 = pss.next()
                            for kc in range(16):
                                k.mm(ps2, ps2[:, 0:128], xj, xj[:, kc, tb * 128:(tb + 1) * 128], wb, wb[:, kc, :], kc == 0, kc == 15)
                            k.copy("act", vtok, vtok[:, tb, c * 128:(c + 1) * 128], ps2, ps2[:, 0:128])
                k.dma("sp", dst_sl(dap), sg_[:], reads=[sg_], writes=[DR[dname]])
                if j == 3:
                    k.dma("sp", Vd[t0:t0 + TW, :].rearrange("(tb p) c -> p tb c", p=128), vtok[:], reads=[vtok], writes=[DR["V"]])
            xj = mix(1, hT)
            for d in range(2):
                ps = pss.next()
                for kc in range(16):
                    k.mm(ps, ps[0:96, 0:TW], lw1, lw1[:, kc, d, :], xj, xj[:, kc, :], kc == 0, kc == 15)
                t_ = th.next()
                k.act(t_, t_[0:96, 0, :], ps, ps[0:96, 0:TW], AF.Tanh)
                sf = stgf.next()
                for c in range(8):
                    ps2 = pss.next()
                    k.mm(ps2, ps2[:, 0:TW], lw2, lw2[:, d, c * 128:(c + 1) * 128], t_, t_[0:96, 0, :], True, True)
                    s_ = sgm.next()
                    k.act(s_, s_[:], ps2, ps2[:, 0:TW], AF.Sigmoid, bias=pct[:, d, c:c + 1], extra_reads=[pct])
                    k.ts("dve", sf, sf[:, c, :], s_, s_[:], -RW_DECAY, ALU.mult)
                k.dma("sp", dst_sl(LWd[d]), sf[:], reads=[sf], writes=[DR["LW"]])
            xj = mix(4, hT)
            for d in range(2):
                ps = pss.next()
                for kc in range(16):
                    k.mm(ps, ps[0:96, 0:TW], la1, la1[:, kc, d, :], xj, xj[:, kc, :], kc == 0, kc == 15)
                t_ = th.next()
                k.copy("act", t_, t_[0:96, 0, :], ps, ps[0:96, 0:TW])
                sg_ = stg.next()
                for c in range(8):
                    ps2 = pss.next()
                    k.mm(ps2, ps2[:, 0:TW], la2, la2[:, d, c * 128:(c + 1) * 128], t_, t_[0:96, 0, :], True, True)
                    k.act(sg_, sg_[:, c, :], ps2, ps2[:, 0:TW], AF.Sigmoid, bias=pct[:, 2 + d, c:c + 1], extra_reads=[pct])
                k.dma("sp", dst_sl(Ad[d]), sg_[:], reads=[sg_], writes=[DR["A"]])
            xj = mix(5, hT)
            t_ = th.next()
            for hc in range(2):
                ps = pss.next()
                for kc in range(16):
                    k.mm(ps, ps[:, 0:TW], lg1, lg1[:, kc, hc * 128:(hc + 1) * 128], xj, xj[:, kc, :], kc == 0, kc == 15)
                k.act(t_, t_[:, hc, :], ps, ps[:, 0:TW], AF.Sigmoid)
            sg_ = stg.next()
            for c in range(8):
                ps2 = pss.next()
                for hc in range(2):
                    k.mm(ps2, ps2[:, 0:TW], lg2, lg2[:, hc, c * 128:(c + 1) * 128], t_, t_[:, hc, :], hc == 0, hc == 1)
                k.copy("act", sg_, sg_[:, c, :], ps2, ps2[:, 0:TW])
            k.dma("sp", dst_sl(Gd), sg_[:], reads=[sg_], writes=[DR["G"]])
        k.pop()
        if stop == 1:
            k.finish()
            return nc
        k.push()
        mk = k.sb([128, 2, 640], F32)
        k.dma("sp", mk[:], msk.rearrange("d s t -> s d t"), writes=[mk])
        epsg = k.sb([128, 1], F32)
        k.memset("dve", epsg, epsg[:], GN_EPS)
        rT = k.sb([128, L], BF16)
        kT = k.sb([128, L], BF16)
        Vtm = k.sb([128, 32, 128], BF16)
        kap = k.sb([128, L], F32)
        OT = k.sb([128, L], F32)
        lw = k.sb([128, L], F32)
        aT = k.sb([128, L], BF16)
        kd = k.sb([128, L], BF16)
        bb = k.sb([128, L], BF16)
        S = RSEG
        NCH = S // 64
        Gs = k.sb([128, S], F32)
        cin = k.sb([128, S], F32)
        cex = k.sb([128, S], F32)
        Ex = k.sb([128, S], F32)
        base = k.sb([128, NCH], F32)
        cl = k.sb([128, NCH], F32)
        gC = k.sb([128, NCH], F32)
        rtm = k.sb([128, 2, S], BF16)
        ktm = k.sb([128, 2, S], BF16)
        kti = k.sb([128, S], BF16)
        bti = k.sb([128, S], BF16)
        ken = k.sb([128, S], BF16)
        ben = k.sb([128, S], BF16)
        AT = [[k.sb([128, 512], BF16) for _ in range(2)] for _ in range(S // 128)]
        XS = [[k.sb([128, 128], BF16) for _ in range(2)] for _ in range(S // 128)]
        KEc = [k.sb([128, 2, 128], BF16) for _ in range(S // 128)]
        BEc = [k.sb([128, 2, 128], BF16) for _ in range(S // 128)]
        for t_ in KEc + BEc:
            k.memset("pool", t_, t_[:], 0.0)
        NTt = [k.sb([128, 128], BF16) for _ in range(4)]
        PPT = [Rot([k.sb([128, 256], BF16) for _ in range(2)]) for _ in range(4)]
        Wsb = [k.sb([128, 64], BF16) for _ in range(2)]
        nU = [k.sb([128, 64], BF16) for _ in range(2)]
        for t_ in Wsb + nU:
            k.memset("pool", t_, t_[:], 0.0)
        H = k.sb([128, 64], F32)
        Hb = k.sb([128, 64], BF16)
        tmpf = Rot([k.sb([128, 512], F32) for _ in range(3)])
        tmpb = Rot([k.sb([128, 512], BF16) for _ in range(4)])
        ob = Rot([k.sb([128, 512], BF16) for _ in range(2)])
        omk = k.sb([128, 8, 2], F32)
        k.ts("dve", omk, omk[:, :, 0], pct, pct[:, 5, :], -1.0, ALU.mult, 1.0, ALU.add)
        k.ts("dve", omk, omk[:, :, 1], pct, pct[:, 5, :], -2.0, ALU.mult, 2.0, ALU.add)

        def c64(buf):
            return buf[:].rearrange("p (n c) -> p n c", c=64)

        def bc(small):
            return small[:].unsqueeze(2).to_broadcast([128, NCH, 64])

        for p in range(8):
            rows = slice(p * 128, (p + 1) * 128)
            k.dma("sp", rT[:], Rd[rows, :], reads=[DR["R"]], writes=[rT])
            k.dma("sp", kT[:], Kdd[rows, :], reads=[DR["K"]], writes=[kT])
            for q4 in range(4):
                k.dma("sp", Vtm[:, q4 * 8:(q4 + 1) * 8, :], Vd[q4 * 1024:(q4 + 1) * 1024, rows].rearrange("(n q) c -> q n c", q=128),
                      reads=[DR["V"]], writes=[Vtm])
            k.ts("dve", kap, kap[:], kT, kT[:], pct[:, 4, p:p + 1], ALU.mult, extra_reads=[pct])
            for t0 in range(0, L, 512):
                sq = tmpb.next()
                k.act(sq, sq[:], kap, kap[:, t0:t0 + 512], AF.Square)
                ps = banks[0]
                k.mm(ps, ps[:], bo1, bo1[:], sq, sq[:], True, True)
                rs_ = tmpf.next()
                k.act(rs_, rs_[:], ps, ps[:], AF.Sqrt)
                k.ts("dve", rs_, rs_[:], rs_, rs_[:], 1e-12, ALU.max)
                k.op("dve", lambda en, rs_=rs_: en.reciprocal(out=rs_[:], in_=rs_[:]), reads=[rs_], writes=[rs_])
                k.tt("dve", kap, kap[:, t0:t0 + 512], kap, kap[:, t0:t0 + 512], rs_, rs_[:], ALU.mult)
            for d in range(2):
                k.dma("sp", lw[:], LWd[d, rows, :], reads=[DR["LW"]], writes=[lw])
                k.dma("sp", aT[:], Ad[d, rows, :], reads=[DR["A"]], writes=[aT])
                for t0 in range(0, L, 512):
                    tf = tmpf.next()
                    k.ts("dve", tf, tf[:], aT, aT[:, t0:t0 + 512], pct[:, 5, p:p + 1], ALU.mult, omk[:, p, 0:1], ALU.add, extra_reads=[pct, omk])
                    k.tt("pool", kd, kd[:, t0:t0 + 512], tf, tf[:], kT, kT[:, t0:t0 + 512], ALU.mult)
                    k.tt("pool", bb, bb[:, t0:t0 + 512], kap, kap[:, t0:t0 + 512], aT, aT[:, t0:t0 + 512], ALU.mult)
                k.memset("dve", H, H[:], 0.0)
                k.memset("dve", Hb, Hb[:], 0.0)
                segs = range(L // S) if d == 0 else range(L // S - 1, -1, -1)
                for sg in segs:
                    so = sg * S
                    sl = slice(so, so + S)
                    of = k._ones_f if hasattr(k, "_ones_f") else None
                    if of is None:
                        of = k.sb([128, 1], F32)
                        k.memset("pool", of, of[:], 1.0)
                        k._ones_f = of
                    k.op("dve", lambda en, sl=sl: en.tensor_tensor_scan(out=Gs[:], data0=of[:, 0:1].to_broadcast([128, S]), data1=lw[:, sl],
                                                                        initial=0.0, op0=ALU.mult, op1=ALU.add), reads=[of, lw], writes=[Gs])
                    lwv = lw[:, sl].rearrange("p (n c) -> p n c", c=64)
                    if d == 0:
                        k.tt("dve", base, base[:], Gs, c64(Gs)[:, :, 0], lw, lwv[:, :, 0], ALU.subtract)
                        k.tt("dve", cin, c64(cin), Gs, c64(Gs), base, bc(base), ALU.subtract)
                        k.tt("dve", cex, cex[:], cin, cin[:], lw, lw[:, sl], ALU.subtract)
                        last = 63
                    else:
                        k.copy("dve", base, base[:], Gs, c64(Gs)[:, :, 63])
                        k.tt("dve", cex, c64(cex), Gs, c64(Gs), base, bc(base), ALU.subtract)
                        k.ts("dve", cex, cex[:], cex, cex[:], -1.0, ALU.mult)
                        k.tt("dve", cin, cin[:], cex, cex[:], lw, lw[:, sl], ALU.add)
                        last = 0
                    k.copy("dve", cl, cl[:], cin, c64(cin)[:, :, last])
                    k.act(gC, gC[:], cl, cl[:], AF.Exp)
                    k.act(Ex, Ex[:], cin, cin[:], AF.Exp)
                    for a in range(2):
                        k.stt(rtm, rtm[:, a, :], rT, rT[:, sl], hm[:, 256 + a:257 + a], Ex, Ex[:], ALU.mult, ALU.mult, extra_reads=[hm])
                    k.act(Ex, Ex[:], cex, cex[:], AF.Exp)
                    for a in range(2):
                        k.stt(ktm, ktm[:, a, :], kap, kap[:, sl], hm[:, 256 + a:257 + a], Ex, Ex[:], ALU.mult, ALU.mult, extra_reads=[hm])
                    k.act(Ex, Ex[:], cin, cin[:], AF.Exp, scale=-1.0)
                    k.tt("dve", kti, kti[:], kd, kd[:, sl], Ex, Ex[:], ALU.mult)
                    k.tt("pool", bti, bti[:], bb, bb[:, sl], Ex, Ex[:], ALU.mult)
                    k.tt("dve", cex, c64(cex), cin, c64(cin), cl, bc(cl), ALU.subtract)
                    k.act(Ex, Ex[:], cex, cex[:], AF.Exp, scale=-1.0)
                    k.tt("dve", ken, ken[:], kd, kd[:, sl], Ex, Ex[:], ALU.mult)
                    k.tt("pool", ben, ben[:], bb, bb[:, sl], Ex, Ex[:], ALU.mult)
                    for n0 in range(0, S // 128, 2):
                        units = [(n0 + i, a) for i in range(2) for a in range(2)]
                        for u, (n, a) in enumerate(units):
                            tk = slice(n * 128, (n + 1) * 128)
                            pa = banks[u]
                            k.mm(pa, pa[:, 0:128], kti, kti[:, tk], ktm, ktm[:, a, tk], True, True)
                            k.mm(pa, pa[:, 128:256], kti, kti[:, tk], rtm, rtm[:, a, tk], True, True)
                            k.mm(pa, pa[:, 256:384], bti, bti[:, tk], ktm, ktm[:, a, tk], True, True)
                            k.mm(pa, pa[:, 384:512], bti, bti[:, tk], rtm, rtm[:, a, tk], True, True)
                            k.tt("dve", AT[n][a], AT[n][a][:], pa, pa[:], mk, mk[:, d, 0:512], ALU.mult)
                            pb_ = banks[4 + (u % 3)]
                            k.mm(pb_, pb_[:, 0:128], ktm, ktm[:, a, tk], bti, bti[:, tk], True, True)
                            k.tt("dve", NTt[u], NTt[u][:], pb_, pb_[:, 0:128], mk, mk[:, d, 512:640], ALU.mult)
                            k.tt("pool", XS[n][a], XS[n][a][:], ident, ident[:], AT[n][a], AT[n][a][:, 256:384], ALU.subtract)
                        cur = []
                        for u, (n, a) in enumerate(units):
                            cur.append((AT[n][a], AT[n][a][:, 256:384], NTt[u], NTt[u][:]))
                        for lvl in range(5):
                            nxt = []
                            for u, (n, a) in enumerate(units):
                                Pb, Pa, PTb, PTa = cur[u]
                                pa = banks[u]
                                k.mm(pa, pa[:, 0:128], PTb, PTa, Pb, Pa, True, True)
                                k.mm(pa, pa[:, 128:256], Pb, Pa, PTb, PTa, True, True)
                                pp = PPT[u].next()
                                k.copy("act" if u % 2 else "dve", pp, pp[:], pa, pa[:, 0:256])
                                nxt.append((pp, pp[:, 0:128], pp, pp[:, 128:256]))
                            for u, (n, a) in enumerate(units):
                                Pb, Pa, PTb, PTa = nxt[u]
                                pa = banks[u]
                                k.mm(pa, pa[:, 256:384], PTb, PTa, XS[n][a], XS[n][a][:], True, True)
                                k.tt("dve" if u % 2 else "pool" if False else "dve", XS[n][a], XS[n][a][:], pa, pa[:, 256:384], XS[n][a], XS[n][a][:], ALU.add)
                            cur = nxt
                        for i in range(2):
                            n = n0 + i
                            tk = slice(n * 128, (n + 1) * 128)
                            for (srcb, dstl) in ((ken, KEc), (ben, BEc)):
                                k.transpose(bankT, bankT[:, 0:128], srcb, srcb[:, tk], ident, ident[:])
                                k.copy("act", dstl[n], dstl[n][0:64, 0, :], bankT, bankT[0:64, 0:128])
                                k.copy("act", dstl[n], dstl[n][64:128, 1, :], bankT, bankT[64:128, 0:128])
                    if stop == 2:
                        k.pop()
                        k.finish()
                        return nc
                    chunks = range(NCH) if d == 0 else range(NCH - 1, -1, -1)
                    for c in chunks:
                        n, hf = c // 2, c % 2
                        ng = sg * (S // 128) + n
                        tkc = slice(c * 64, (c + 1) * 64)
                        tp = slice(64 * hf, 64 * hf + 64)
                        ch = slice(64 * hf, 64 * hf + 64)
                        Ops, dHps = banks[4], banks[5]
                        for a in range(2):
                            hs = slice(64 * a, 64 * a + 64)
                            Wps, Ups = banks[a], banks[2 + a]
                            At = AT[n][a]
                            k.mm(Wps, Wps[tp, 0:64], ktm, ktm[:, a, tkc], Hb, Hb[:], True, False)
                            k.mm(Wps, Wps[tp, 0:64], At, At[:, 0 + 64 * hf:64 + 64 * hf], Vtm, Vtm[:, ng, hs], False, True)
                            k.copy("dve", Wsb[a], Wsb[a][tp, :], Wps, Wps[tp, 0:64])
                            k.mm(Ups, Ups[tp, 0:64], XS[n][a], XS[n][a][:, ch], Wsb[a], Wsb[a][:], True, True)
                            k.act(nU[a], nU[a][tp, :], Ups, Ups[tp, 0:64], AF.Copy, scale=-1.0)
                            k.mm(Ops, Ops[hs, 0:64], Hb, Hb[:], rtm, rtm[:, a, tkc], True, False)
                            k.mm(Ops, Ops[hs, 0:64], Vtm, Vtm[:, ng, hs], At, At[:, 128 + 64 * hf:192 + 64 * hf], False, False)
                            k.mm(Ops, Ops[hs, 0:64], nU[a], nU[a][:], At, At[:, 384 + 64 * hf:448 + 64 * hf], False, True)
                            k.mm(dHps, dHps[hs, 0:64], KEc[n], KEc[n][:, hf, hs], Vtm, Vtm[:, ng, hs], True, False)
                            k.mm(dHps, dHps[hs, 0:64], BEc[n], BEc[n][:, hf, hs], nU[a], nU[a][:], False, True)
                        osl = slice(so + c * 64, so + (c + 1) * 64)
                        if d == 0:
                            k.copy("act", OT, OT[:, osl], Ops, Ops[:, 0:64])
                        else:
                            k.tt("dve", OT, OT[:, osl], Ops, Ops[:, 0:64], OT, OT[:, osl], ALU.add)
                        k.stt(H, H[:], H, H[:], gC[:, c:c + 1], dHps, dHps[:, 0:64], ALU.mult, ALU.add, extra_reads=[gC])
                        k.copy("act", Hb, Hb[:], H, H[:])
            if stop == 3:
                k.pop()
                k.finish()
                return nc
            for t0 in range(0, L, 512):
                ts_ = slice(t0, t0 + 512)
                obf = tmpb.next()
                k.copy("act", obf, obf[:], OT, OT[:, ts_])
                o2 = tmpb.next()
                k.act(o2, o2[:], OT, OT[:, ts_], AF.Square)
                pm, pq, pbn = banks[0], banks[1], banks[2]
                k.mm(pm, pm[:], bo64, bo64[:], obf, obf[:], True, True)
                k.mm(pq, pq[:], bo64, bo64[:], o2, o2[:], True, True)
                mean = tmpf.next()
                k.copy("act", mean, mean[:], pm, pm[:])
                var = tmpf.next()
                k.tt("pool", var, var[:], mean, mean[:], mean, mean[:], ALU.mult)
                k.tt("dve", var, var[:], pq, pq[:], var, var[:], ALU.subtract)
                k.ts("dve", var, var[:], var, var[:], 0.0, ALU.max)
                k.act(var, var[:], var, var[:], AF.Sqrt, bias=epsg[:, 0:1], extra_reads=[epsg])
                k.op("dve", lambda en, var=var: en.reciprocal(out=var[:], in_=var[:]), reads=[var], writes=[var])
                k.tt("dve", mean, mean[:], OT, OT[:, ts_], mean, mean[:], ALU.subtract)
                k.tt("dve", mean, mean[:], mean, mean[:], var, var[:], ALU.mult)
                k.ts("dve", mean, mean[:], mean, mean[:], pct[:, 7, p:p + 1], ALU.mult, pct[:, 8, p:p + 1], ALU.add, extra_reads=[pct])
                a0 = tmpb.next()
                a1_ = tmpb.next()
                k.dma("sp", a0[:], Ad[0, rows, ts_], reads=[DR["A"]], writes=[a0])
                k.dma("sp", a1_[:], Ad[1, rows, ts_], reads=[DR["A"]], writes=[a1_])
                asum = tmpf.next()
                k.tt("pool", asum, asum[:], a0, a0[:], a1_, a1_[:], ALU.add)
                k.ts("dve", asum, asum[:], asum, asum[:], pct[:, 5, p:p + 1], ALU.mult, omk[:, p, 1:2], ALU.add, extra_reads=[pct, omk])
                k.tt("pool", asum, asum[:], asum, asum[:], kT, kT[:, ts_], ALU.mult)
                rk = tmpb.next()
                k.stt(rk, rk[:], rT, rT[:, ts_], pct[:, 6, p:p + 1], asum, asum[:], ALU.mult, ALU.mult, extra_reads=[pct])
                k.mm(pbn, pbn[:], bo1, bo1[:], rk, rk[:], True, True)
                vt_ = tmpb.next()
                k.dma("sp", vt_[:], VTd[rows, ts_], reads=[DR["VT"]], writes=[vt_])
                k.tt("dve", var, var[:], pbn, pbn[:], vt_, vt_[:], ALU.mult)
                k.tt("pool", mean, mean[:], mean, mean[:], var, var[:], ALU.add)
                gt_ = tmpb.next()
                k.dma("sp", gt_[:], Gd[rows, ts_], reads=[DR["G"]], writes=[gt_])
                o_ = ob.next()
                k.tt("dve", o_, o_[:], mean, mean[:], gt_, gt_[:], ALU.mult)
                k.dma("sp", yT[rows, ts_], o_[:], reads=[o_], is_output=True)
            if stop == 4:
                break
        k.pop()
        k.finish()
    return nc


def run_rwkv(xT_full, inputs, mod):
    import os
    stop = int(os.environ.get("RW_STOP", "0"))
    ncore = int(os.environ.get("RW_CORES", "8"))
    nc = build_rwkv(stop)
    s_ = np.arange(128)
    bones = ((s_[:, None] // 64) == (s_[None, :] // 64)).astype(np.float32)
    hmask = np.stack([(s_ // 64 == 0), (s_ // 64 == 1)], axis=1).astype(np.float32)
    cst = np.ascontiguousarray(np.concatenate([bones, np.eye(128, dtype=np.float32), hmask], axis=1))
    msk = rwkv_masks()
    mu = np.ascontiguousarray(np.stack([to_pv(inputs["rw_mu"][0][j]) for j in range(6)], axis=1))
    w1 = np.ascontiguousarray(inputs["rw_w1"][0].transpose(1, 0, 2))
    a1 = np.ascontiguousarray(inputs["rw_a1"][0].transpose(1, 0, 2))
    in_maps = []
    for j in range(8):
        b, s = j // 2, j % 2
        cols = slice(1024 * s, 1024 * s + 1024)
        wrkv = np.stack([inputs["rw_w_r"][0][:, cols], inputs["rw_w_k"][0][:, cols], inputs["rw_w_v"][0][:, cols]])
        vecs = [inputs["rw_w0"][0][0], inputs["rw_w0"][0][1], inputs["rw_a0"][0][0], inputs["rw_a0"][0][1], inputs["rw_k_k"][0],
                inputs["rw_k_a"][0], inputs["rw_r_k"][0].reshape(-1), inputs["rw_gn_g"][0], inputs["rw_gn_b"][0]]
        pc = np.stack([v[cols].reshape(8, 128).T for v in vecs], axis=1)
        pv = np.stack([to_pv(inputs["tm_norm_g"][0]), to_pv(mod[b, 2048:4096]), to_pv(mod[b, 0:2048])], axis=1)
        in_maps.append({"xT": xT_full[b], "pv": np.ascontiguousarray(pv), "mu": mu, "wrkv": np.ascontiguousarray(wrkv),
                        "w1": w1, "a1": a1, "g1": inputs["rw_g1"][0],
                        "w2": np.ascontiguousarray(inputs["rw_w2"][0][:, :, cols].transpose(1, 0, 2)),
                        "a2": np.ascontiguousarray(inputs["rw_a2"][0][:, :, cols].transpose(1, 0, 2)),
                        "g2": np.ascontiguousarray(inputs["rw_g2"][0][:, cols]),
                        "pc": np.ascontiguousarray(pc.astype(np.float32)), "msk": msk, "cst": cst})
    res = run_bass_kernel_spmd(nc, in_maps[:ncore], core_ids=list(range(ncore)))
    out = np.zeros((4, 2048, L), NPBF)
    for j in range(ncore):
        b, s = j // 2, j % 2
        out[b, 1024 * s:1024 * s + 1024] = res.results[j]["yT"]
    return out


MISSING = ()


def kernel(**inputs):
    if MISSING:
        raise NotImplementedError(f"kernel is incomplete: mixer launches {MISSING} are not implemented "
                                  "(implemented+verified: adaLN mods, GLA mixer, dense out-proj/FFN launch)")
    inputs = {k_: np.asarray(v) for k_, v in inputs.items()}
    x = inputs["x"].astype(np.float32, copy=False)
    xT = np.ascontiguousarray(x.transpose(0, 2, 1))
    mods = run_mod(inputs)
    y_gla = run_gla(xT, inputs, mods[0])
    y_hy = run_hyena(xT, inputs, mods[0])
    ymT = np.concatenate([y_hy, y_gla], axis=1)
    vecs = [[mods[0][b, 4096:], inputs["ffn_norm_g"][0], mods[1][b, 2048:4096], mods[1][b, :2048], mods[1][b, 4096:],
             inputs["final_norm_g"]] for b in range(4)]
    xT = run_dense(0, xT, ymT, inputs["ab_w_out"][0], inputs["ffn_w1"][0], inputs["ffn_w2"][0], vecs)
    y_rw = run_rwkv(xT, inputs, mods[2])
    vecs = [[mods[2][b, 4096:], inputs["ffn_norm_g"][1], mods[3][b, 2048:4096], mods[3][b, :2048], mods[3][b, 4096:],
             inputs["final_norm_g"]] for b in range(4)]
    xT = run_dense(1, xT, y_rw, inputs["rw_w_o"][0], inputs["ffn_w1"][1], inputs["ffn_w2"][1], vecs)
    return np.ascontiguousarray(xT.transpose(0, 2, 1)).astype(np.float32)
```
